# Optimizing a Trainium2 kernel written in Bass

```python
import math
import jax, jax.numpy as jnp
from jax import lax
import numpy as np

D_MODEL = 1024
BATCH = 8
SEQ = 2048
DEPTH = 2

CTX_LEN = 256
GRID_W = 64
MIX_WIDTH = D_MODEL
ATTN_WIDTH = MIX_WIDTH // 2
FOURIER_WIDTH = MIX_WIDTH - ATTN_WIDTH
HEAD_DIM = 64
N_HEADS = ATTN_WIDTH // (2 * HEAD_DIM)
V_HEAD_DIM = 2 * HEAD_DIM
N_FGROUPS = 4
FGROUP_DIM = FOURIER_WIDTH // N_FGROUPS
IN_COLS = 2 * ATTN_WIDTH + N_HEADS * V_HEAD_DIM + FOURIER_WIDTH
D_FF = -(-8 * D_MODEL // (3 * 256)) * 256
ROPE_AXIS_DIM = HEAD_DIM // 2
ROPE_THETA = 10000.0
Q_BLOCK = 128
EPS = 1e-6

kernel_name = "hybrid_diffattn_fnet_dit_block"


def rms_norm(x, gain):
    xf = x.astype(jnp.float32)
    y = xf * lax.rsqrt(jnp.mean(xf * xf, axis=-1, keepdims=True) + EPS)
    return (y * gain.astype(jnp.float32)).astype(x.dtype)


def modulate(h, shift, scale):
    return h * (1 + scale) + shift


def axial_rope_tables(n_tokens):
    rows = n_tokens // GRID_W
    row = jnp.broadcast_to(jnp.arange(rows, dtype=jnp.float32)[:, None], (rows, GRID_W)).reshape(-1)
    col = jnp.broadcast_to(jnp.arange(GRID_W, dtype=jnp.float32)[None, :], (rows, GRID_W)).reshape(-1)
    inv = ROPE_THETA ** (-jnp.arange(0, ROPE_AXIS_DIM, 2, dtype=jnp.float32) / ROPE_AXIS_DIM)
    ang = jnp.concatenate([row[:, None] * inv, col[:, None] * inv], axis=-1)
    return jnp.cos(ang), jnp.sin(ang)


def apply_axial_rope(t, cos, sin):
    tf = t.astype(jnp.float32)
    c = cos[:, None, None, :]
    s = sin[:, None, None, :]
    half = ROPE_AXIS_DIM // 2

    def rot(ta, ca, sa):
        t1, t2 = ta[..., :half], ta[..., half:]
        return jnp.concatenate([t1 * ca - t2 * sa, t2 * ca + t1 * sa], axis=-1)

    out = jnp.concatenate([
        rot(tf[..., :ROPE_AXIS_DIM], c[..., :half], s[..., :half]),
        rot(tf[..., ROPE_AXIS_DIM:], c[..., half:], s[..., half:]),
    ], axis=-1)
    return out.astype(t.dtype)


def split_proj(h, w_in, q_gain, k_gain):
    B, L, _ = h.shape
    z = h @ w_in
    q, k, v, f = jnp.split(z, [ATTN_WIDTH, 2 * ATTN_WIDTH, 2 * ATTN_WIDTH + N_HEADS * V_HEAD_DIM], axis=-1)
    q = rms_norm(q.reshape(B, L, N_HEADS, 2, HEAD_DIM), q_gain)
    k = rms_norm(k.reshape(B, L, N_HEADS, 2, HEAD_DIM), k_gain)
    v = v.reshape(B, L, N_HEADS, V_HEAD_DIM)
    return q, k, v, f


def to_heads_qk(t):
    return t.transpose(0, 2, 3, 1, 4)


def to_heads_v(t):
    return t.transpose(0, 2, 1, 3)


def diff_attend(q, k, v, lam):
    s = jnp.einsum('bhiqd,bhikd->bhiqk', q.astype(jnp.float32), k.astype(jnp.float32)) * (HEAD_DIM ** -0.5)
    p = jax.nn.softmax(s, axis=-1)
    a = p[:, :, 0] - lam * p[:, :, 1]
    return jnp.einsum('bhqk,bhkv->bhqv', a, v.astype(jnp.float32)).astype(v.dtype)


def blocked_diff_attend(q, k, v, lam):
    B, H, _, L, d = q.shape
    nb = L // Q_BLOCK
    qb = q.reshape(B, H, 2, nb, Q_BLOCK, d).transpose(3, 0, 1, 2, 4, 5)
    out = lax.map(lambda qblk: diff_attend(qblk, k, v, lam), qb)
    return out.transpose(1, 2, 0, 3, 4).reshape(B, H, L, V_HEAD_DIM)


def heads_out(o, subln_gain, lambda_init):
    o = rms_norm(o, subln_gain) * (1.0 - lambda_init)
    B, H, L, Dv = o.shape
    return o.transpose(0, 2, 1, 3).reshape(B, L, H * Dv)


def fourier_mix(f, w_f):
    B, L, _ = f.shape
    fg = f.reshape(B, L, N_FGROUPS, FGROUP_DIM).astype(jnp.float32)
    spec = jnp.fft.fft2(fg, axes=(1, 3), norm='ortho').real.astype(f.dtype)
    return jnp.einsum('blgc,gcd->blgd', spec, w_f).reshape(B, L, FOURIER_WIDTH)


def swiglu(h, w_gate, w_up, w_down):
    return (jax.nn.silu(h @ w_gate) * (h @ w_up)) @ w_down


def setup_inputs(seed: int = 0) -> dict:
    key = jax.random.key(seed)
    ks = jax.random.split(key, 24)
    f32 = jnp.float32
    nrm = lambda k, shape, s: jax.random.normal(k, shape, f32) * s
    D = D_MODEL
    return {
        'x': nrm(ks[0], (BATCH, SEQ, D), 1.0),
        'c': nrm(ks[1], (BATCH, D), 1.0),
        'ctx': nrm(ks[2], (BATCH, CTX_LEN, D), 1.0),
        'c_ctx': nrm(ks[3], (D,), 1.0),
        'w_ada': nrm(ks[4], (DEPTH, D, 6 * D), 0.5 * D ** -0.5),
        'b_ada': nrm(ks[5], (DEPTH, 6 * D), 0.01),
        'norm1_g': 1.0 + nrm(ks[6], (DEPTH, D), 0.02),
        'norm2_g': 1.0 + nrm(ks[7], (DEPTH, D), 0.02),
        'w_in': nrm(ks[8], (DEPTH, D, IN_COLS), D ** -0.5),
        'q_norm_g': 1.0 + nrm(ks[9], (DEPTH, HEAD_DIM), 0.02),
        'k_norm_g': 1.0 + nrm(ks[10], (DEPTH, HEAD_DIM), 0.02),
        'lambda_q1': nrm(ks[11], (DEPTH, HEAD_DIM), 0.1),
        'lambda_k1': nrm(ks[12], (DEPTH, HEAD_DIM), 0.1),
        'lambda_q2': nrm(ks[13], (DEPTH, HEAD_DIM), 0.1),
        'lambda_k2': nrm(ks[14], (DEPTH, HEAD_DIM), 0.1),
        'subln_g': 1.0 + nrm(ks[15], (DEPTH, V_HEAD_DIM), 0.02),
        'w_fourier': nrm(ks[16], (DEPTH, N_FGROUPS, FGROUP_DIM, FGROUP_DIM), FGROUP_DIM ** -0.5),
        'w_out': nrm(ks[17], (DEPTH, MIX_WIDTH, D), MIX_WIDTH ** -0.5),
        'w_gate': nrm(ks[18], (DEPTH, D, D_FF), D ** -0.5),
        'w_up': nrm(ks[19], (DEPTH, D, D_FF), D ** -0.5),
        'w_down': nrm(ks[20], (DEPTH, D_FF, D), D_FF ** -0.5),
    }


def reference(x, c, ctx, c_ctx, w_ada, b_ada, norm1_g, norm2_g, w_in, q_norm_g, k_norm_g,
              lambda_q1, lambda_k1, lambda_q2, lambda_k2, subln_g, w_fourier, w_out,
              w_gate, w_up, w_down):
    L = x.shape[1]
    cos, sin = axial_rope_tables(L)
    for i in range(DEPTH):
        last = i == DEPTH - 1
        lambda_init = 0.8 - 0.6 * math.exp(-0.3 * i)
        lam = (jnp.exp(jnp.sum(lambda_q1[i].astype(jnp.float32) * lambda_k1[i].astype(jnp.float32)))
               - jnp.exp(jnp.sum(lambda_q2[i].astype(jnp.float32) * lambda_k2[i].astype(jnp.float32)))
               + lambda_init)

        mod_x = jax.nn.silu(c) @ w_ada[i] + b_ada[i]
        mod_c = jax.nn.silu(c_ctx) @ w_ada[i] + b_ada[i]
        sh_a, sc_a, g_a, sh_f, sc_f, g_f = [m[:, None, :] for m in jnp.split(mod_x, 6, axis=-1)]
        csh_a, csc_a, cg_a, csh_f, csc_f, cg_f = jnp.split(mod_c, 6, axis=-1)

        hc = modulate(rms_norm(ctx, norm1_g[i]), csh_a, csc_a)
        qc, kc, vc, fc = split_proj(hc, w_in[i], q_norm_g[i], k_norm_g[i])
        kc_h, vc_h = to_heads_qk(kc), to_heads_v(vc)

        hx = modulate(rms_norm(x, norm1_g[i]), sh_a, sc_a)
        qx, kx, vx, fx = split_proj(hx, w_in[i], q_norm_g[i], k_norm_g[i])
        qx = apply_axial_rope(qx, cos, sin)
        kx = apply_axial_rope(kx, cos, sin)
        k_all = jnp.concatenate([kc_h, to_heads_qk(kx)], axis=3)
        v_all = jnp.concatenate([vc_h, to_heads_v(vx)], axis=2)
        attn_x = heads_out(blocked_diff_attend(to_heads_qk(qx), k_all, v_all, lam), subln_g[i], lambda_init)
        four_x = fourier_mix(fx, w_fourier[i])
        mix_x = jnp.concatenate([attn_x, four_x], axis=-1) @ w_out[i]
        x_new = x + g_a * mix_x
        hx2 = modulate(rms_norm(x_new, norm2_g[i]), sh_f, sc_f)
        x_new = x_new + g_f * swiglu(hx2, w_gate[i], w_up[i], w_down[i])

        if not last:
            attn_c = heads_out(diff_attend(to_heads_qk(qc), kc_h, vc_h, lam), subln_g[i], lambda_init)
            four_c = fourier_mix(fc, w_fourier[i])
            mix_c = jnp.concatenate([attn_c, four_c], axis=-1) @ w_out[i]
            ctx = ctx + cg_a * mix_c
            hc2 = modulate(rms_norm(ctx, norm2_g[i]), csh_f, csc_f)
            ctx = ctx + cg_f * swiglu(hc2, w_gate[i], w_up[i], w_down[i])
        x = x_new
    return x
```

```python
import math
import numpy as np
import ml_dtypes
import concourse.bass as bass
import concourse.mybir as mybir
from concourse.bass_utils import run_bass_kernel_spmd

F32 = mybir.dt.float32
BF16 = mybir.dt.bfloat16
AF = mybir.ActivationFunctionType
ALU = mybir.AluOpType

D = 1024
L = 2048
LC = 256
T = L + LC
KC = 8
DFF = 2816
NFF = 22
DEPTH = 2
EPS = 1e-6
BLOCKS = [(0, 256), (256, 512), (768, 512), (1280, 512), (1792, 512)]
NV = 154
NSLOT = 4
LOOKAHEAD = 3
EPOCH = 4000


def lambda_init(i):
    return 0.8 - 0.6 * math.exp(-0.3 * i)


def blk_of_tile(t):
    return 0 if t < 2 else 1 + (t - 2) // 4


class StopBuild(Exception):
    pass


class Op:
    __slots__ = ("eng", "fn", "deps", "signal", "count", "is_dma", "stream")


class Sched:
    ENGS = ["pe", "act", "dve", "pool", "sp"]

    def __init__(self):
        self.ops = {e: [] for e in self.ENGS}
        self.lastw = {}
        self.readers = {}
        self.dma_count = {}

    def op(self, eng, fn, r=(), w=(), stream=None):
        o = Op()
        o.eng = eng
        o.fn = fn
        o.signal = False
        o.count = 0
        o.is_dma = stream is not None
        o.stream = stream
        deps = {}

        def add(d):
            if d is o:
                return
            if (not d.is_dma) and (not o.is_dma) and d.eng == "pe" and eng == "pe":
                return
            deps[id(d)] = d

        for k in r:
            d = self.lastw.get(k)
            if d is not None:
                add(d)
            if k[0] == "ps":
                rd = self.readers.get(k)
                if rd:
                    for e2, d2 in rd[0].items():
                        if e2 != eng:
                            add(d2)
        for k in w:
            d = self.lastw.get(k)
            if d is not None:
                add(d)
            rd = self.readers.get(k)
            if rd:
                for d in rd[0].values():
                    add(d)
                for d in rd[1]:
                    add(d)
        for k in r:
            rd = self.readers.setdefault(k, ({}, []))
            if o.is_dma:
                rd[1].append(o)
            else:
                rd[0][eng] = o
        for k in w:
            self.lastw[k] = o
            self.readers[k] = ({}, [])
        o.deps = list(deps.values())
        for d in o.deps:
            d.signal = True
        if o.is_dma:
            c = self.dma_count.get(stream, 0) + 16
            self.dma_count[stream] = c
            o.count = c
        self.ops[eng].append(o)
        return o

    def emit(self, nc, block):
        esems = {}
        dsems = {}
        for e in self.ENGS:
            c = 0
            for o in self.ops[e]:
                if (not o.is_dma) and o.signal:
                    c += 1
                    o.count = c
            nep = (c + EPOCH - 1) // EPOCH + 1
            esems[e] = [nc.alloc_semaphore("se_%s_%d" % (e, i)) for i in range(nep)]
        for i, s in enumerate(self.dma_count.keys()):
            dsems[s] = nc.alloc_semaphore("sd_%d" % i)

        def semval(d):
            if d.is_dma:
                return dsems[d.stream], d.count
            ep = (d.count - 1) // EPOCH
            return esems[d.eng][ep], d.count - ep * EPOCH

        def run(e, h):
            waited = {}
            for o in self.ops[e]:
                for d in o.deps:
                    sem, val = semval(d)
                    key = id(sem)
                    if waited.get(key, 0) < val:
                        h.wait_ge(sem, val)
                        waited[key] = val
                inst = o.fn(h)
                if inst is None:
                    continue
                if o.is_dma:
                    inst.then_inc(dsems[o.stream], 16)
                elif o.signal:
                    sem, _ = semval(o)
                    inst.then_inc(sem, 1)

        block.tensor(lambda h: run("pe", h))
        block.scalar(lambda h: run("act", h))
        block.vector(lambda h: run("dve", h))
        block.gpsimd(lambda h: run("pool", h))
        block.sync(lambda h: run("sp", h))


class WBM:
    def __init__(self, S, future=None):
        self.S = S
        self.future = future
        self.rec = []
        self.pos = 0
        self.issued = 0
        self.free = list(range(NSLOT))
        self.slot_of = {}

    def _issue(self):
        i = self.issued
        slot = self.free.pop(0)
        self.future[i](self.S, slot)
        self.slot_of[i] = slot
        self.issued += 1

    def _pump(self):
        while self.free and self.issued < min(len(self.future), self.pos + LOOKAHEAD):
            self._issue()

    def request(self, loader):
        if self.future is None:
            self.rec.append(loader)
            return 0
        i = self.pos
        self.pos += 1
        while self.issued <= i:
            assert self.free, "no free weight slot"
            self._issue()
        self._pump()
        return self.slot_of[i]

    def release(self, slot):
        if self.future is None:
            return
        self.free.append(slot)
        self._pump()


def build_program(debug=None, stop_after=None):
    nc = bass.Bass("TRN2", target_bir_lowering=False)
    dr = {}

    def din(name, shape, dt):
        dr[name] = nc.dram_tensor(name, list(shape), dt, kind="ExternalInput").ap()
        return dr[name]

    xT_d = din("xT", [D, L], F32)
    cT_d = din("cT", [D, LC], F32)
    vecs_d = din("vecs", [128, NV], F32)
    lamin_d = din("lamin", [128, 512], F32)
    cossin_d = din("cossin", [2, 128, L], F32)
    dftL_d = din("dftL", [2, L, L // 2], BF16)
    dftC_d = din("dftC", [128, 2 * 2 * 256], BF16)
    cmat_d = din("cmat", [128, 784], BF16)
    w_ada_d = din("w_ada", [DEPTH, D, 6 * D], F32)
    w_in_d = din("w_in", [DEPTH, D, 2048], F32)
    w_out_d = din("w_out", [DEPTH, D, D], F32)
    w_gate_d = din("w_gate", [DEPTH, D, DFF], F32)
    w_up_d = din("w_up", [DEPTH, D, DFF], F32)
    w_down_d = din("w_down", [DEPTH, DFF, D], F32)
    w_f_d = din("w_fourier", [DEPTH, 4, 128, 128], F32)
    yT_d = nc.dram_tensor("yT", [D, L], F32, kind="ExternalOutput").ap()

    XT = nc.alloc_sbuf_tensor("XT", [128, KC * T], F32)
    HT = nc.alloc_sbuf_tensor("HT", [128, KC * T], BF16)
    WBt = nc.alloc_sbuf_tensor("WB", [128, NSLOT * 4096], BF16)
    R = nc.alloc_sbuf_tensor("R", [128, 30208], BF16)
    SM = nc.alloc_sbuf_tensor("SM", [128, 512], F32)
    ZB = nc.alloc_sbuf_tensor("ZB", [128, 1024], BF16)
    CM = nc.alloc_sbuf_tensor("CM", [128, 800], BF16)
    SUB = nc.alloc_sbuf_tensor("SUB", [128, 1536], BF16)
    PS = nc.alloc_psum_tensor("PS", [128, 8, 512], F32)

    xT3 = XT[:, :].rearrange("p (k t) -> p k t", k=KC)
    hT3 = HT[:, :].rearrange("p (k t) -> p k t", k=KC)

    def wslot(s):
        return WBt[:, s * 4096:(s + 1) * 4096]

    def Rb(a, n):
        return R[:, a:a + n]

    def Rf(a, n):
        return R[:, a:a + 2 * n].bitcast(F32)

    vecs = SM[:, 0:NV]
    mod4 = SM[:, 160:160 + 192].rearrange("p (l c i) -> p l c i", l=2, c=48, i=2)
    a1v = SM[:, 352:384].rearrange("p (l k i) -> p l k i", l=2, k=8, i=2)
    a2v = SM[:, 384:416].rearrange("p (l k i) -> p l k i", l=2, k=8, i=2)
    neglam = SM[:, 416:418]
    sgc = SM[:, 418:420]
    lam_s = SM[:, 420:428]
    ones_m = CM[:, 0:128]
    bones_m = CM[:, 128:256]
    cc_m = CM[:, 256:384]
    scn_m = CM[:, 384:512]
    scp_m = CM[:, 512:640]
    sgn_m = CM[:, 640:656]
    pm_m = CM[:, 656:784]
    siluc3 = CM[:, 784:800].rearrange("p (k i) -> p k i", k=8, i=2)

    def vcol(l, j):
        return vecs[:, 69 * l + j:69 * l + j + 1]

    def ps(b, n=512):
        return PS[:, b, 0:n]

    state = {"bank": 0}

    def nb():
        b = state["bank"]
        state["bank"] = (b + 1) % 8
        return b

    def build(S, W):
        def mm(out, lhsT, rhs, start, stop, r, w):
            S.op("pe", lambda e: e.matmul(out, lhsT, rhs, start=start, stop=stop), r, w)

        def act(out, in_, func, r, w, scale=1.0, bias=0.0):
            S.op("act", lambda e: e.activation(out=out, in_=in_, func=func, bias=bias, scale=scale), r, w)

        def stt(out, in0, scalar, in1, op0, op1, r, w):
            S.op("dve", lambda e: e.scalar_tensor_tensor(out=out, in0=in0, scalar=scalar, in1=in1, op0=op0, op1=op1), r, w)

        def tt(out, in0, in1, op, r, w):
            S.op("dve", lambda e: e.tensor_tensor(out=out, in0=in0, in1=in1, op=op), r, w)

        def dcopy(out, in_, r, w):
            S.op("dve", lambda e: e.tensor_copy(out=out, in_=in_), r, w)

        evac_flip = {"i": 0}

        def evac(out, in_, r, w):
            evac_flip["i"] ^= 1
            if evac_flip["i"]:
                S.op("act", lambda e: e.activation(out=out, in_=in_, func=AF.Copy), r, w)
            else:
                dcopy(out, in_, r, w)

        bar = {"keys": []}

        def barrier():
            S.op("pe", lambda e: e.matmul(PS[0:1, 7, 511:512], ones_m[:, 0:1], ones_m[:, 0:1], start=True, stop=True),
                 [("cmat",)], [("bar1", "pe"), ("ps", 7)])
            S.op("act", lambda e: e.activation(out=SM[:, 432:433], in_=SM[:, 430:431], func=AF.Copy), [("eps",)], [("bar1", "act")])
            S.op("dve", lambda e: e.tensor_copy(out=SM[:, 433:434], in_=SM[:, 430:431]), [("eps",)], [("bar1", "dve")])
            b1 = [("bar1", "pe"), ("bar1", "act"), ("bar1", "dve")]
            S.op("pe", lambda e: e.matmul(PS[0:1, 7, 511:512], ones_m[:, 0:1], ones_m[:, 0:1], start=True, stop=True),
                 b1 + [("cmat",)], [("bar2", "pe"), ("ps", 7)])
            S.op("act", lambda e: e.activation(out=SM[:, 434:435], in_=SM[:, 430:431], func=AF.Copy), b1 + [("eps",)], [("bar2", "act")])
            S.op("dve", lambda e: e.tensor_copy(out=SM[:, 435:436], in_=SM[:, 430:431]), b1 + [("eps",)], [("bar2", "dve")])
            bar["keys"] = [("bar2", "pe"), ("bar2", "act"), ("bar2", "dve")]

        def rsqrt_from_psum(dst, src, rkeys, wkey):
            act(dst, src, AF.Ln, rkeys, [wkey], scale=1.0, bias=EPS_AP)
            act(dst, dst, AF.Exp, [wkey], [wkey], scale=-0.5)

        def wload(src_ap_fn, shape3, cast=True):
            def loader(S_, slot):
                k, c = shape3
                dst = wslot(slot)[:, 0:k * c].rearrange("p (k c) -> p k c", k=k)
                eng = "pool" if cast else "sp"
                S_.op(eng, lambda e: e.dma_start(out=dst, in_=src_ap_fn()), [],
                      [("wb", slot), ("wbsw", slot)], stream=("wb", slot, eng))
            return loader

        S.op("sp", lambda e: e.dma_start(out=vecs, in_=vecs_d[:, :]), [], [("vecs",)], stream=("c", 0))
        S.op("sp", lambda e: e.dma_start(out=CM[:, 0:784], in_=cmat_d[:, :]), [], [("cmat",)], stream=("c", 1))
        lamv = Rf(0, 512)
        S.op("sp", lambda e: e.dma_start(out=xT3[:, :, 0:LC], in_=cT_d.rearrange("(k p) t -> p k t", p=128)),
             [], [("xT", k, 0) for k in range(KC)], stream=("xc", 0))
        for bi in range(1, 5):
            s0 = BLOCKS[bi][0]
            S.op("sp", (lambda bi, s0: lambda e: e.dma_start(
                out=xT3[:, :, s0:s0 + 512],
                in_=xT_d.rearrange("(k p) t -> p k t", p=128)[:, :, s0 - LC:s0 - LC + 512]))(bi, s0),
                [], [("xT", k, bi) for k in range(KC)], stream=("xl", bi))
        S.op("sp", lambda e: e.dma_start(out=lamv, in_=lamin_d[:, :]), [], [("lamin",)], stream=("c", 2))

        S.op("dve", lambda e: e.memset(SM[:, 430:431], EPS), [], [("eps",)])
        for i in range(2):
            act(siluc3[:, :, i], vecs[:, 138 + 8 * i:146 + 8 * i], AF.Silu, [("vecs",)], [("siluc", i)])
        lam4 = lamv.rearrange("p (l f d) -> p l f d", l=2, f=4, d=64)
        tmpl = Rf(1024, 64)
        for l in range(DEPTH):
            for j in range(2):
                tt(tmpl, lam4[:, l, 2 * j, :], lam4[:, l, 2 * j + 1, :], ALU.mult, [("lamin",)], [("tmpl",)])
                S.op("dve", (lambda l, j: lambda e: e.reduce_sum(out=lam_s[:, 2 * l + j:2 * l + j + 1], in_=tmpl,
                                                                axis=mybir.AxisListType.X))(l, j),
                     [("tmpl",)], [("lams", l, j)])
            act(lam_s[:, 4 + 2 * l:6 + 2 * l], lam_s[:, 2 * l:2 * l + 2], AF.Exp,
                [("lams", l, 0), ("lams", l, 1)], [("lame", l)])
            tt(neglam[:, l:l + 1], lam_s[:, 5 + 2 * l:6 + 2 * l], lam_s[:, 4 + 2 * l:5 + 2 * l], ALU.subtract,
               [("lame", l)], [("neglam", l)])
            S.op("dve", (lambda l: lambda e: e.tensor_scalar(out=neglam[:, l:l + 1], in0=neglam[:, l:l + 1],
                                                             scalar1=-lambda_init(l), scalar2=None, op0=ALU.add))(l),
                 [("neglam", l)], [("neglam", l)])
            S.op("dve", (lambda l: lambda e: e.tensor_scalar(out=sgc[:, l:l + 1], in0=vcol(l, 68),
                                                             scalar1=1.0 - lambda_init(l), scalar2=None, op0=ALU.mult))(l),
                 [("vecs",)], [("sgc", l)])

        def ada_piece(l, j):
            slot = W.request(wload(lambda: w_ada_d[l, :, j * 512:(j + 1) * 512].rearrange("(k p) c -> p k c", p=128), (8, 512)))
            s3 = wslot(slot).rearrange("p (k c) -> p k c", k=8)
            b = nb()
            for fc in range(4):
                for k in range(KC):
                    mm(PS[:, b, 2 * fc:2 * fc + 2], s3[:, k, fc * 128:(fc + 1) * 128], siluc3[:, k, :],
                       k == 0, k == KC - 1, [("wb", slot), ("siluc", 0), ("siluc", 1)], [("ps", b)])
            pv = PS[:, b, 0:8].rearrange("p (f i) -> p f i", i=2)
            for i in range(2):
                tt(mod4[:, l, 4 * j:4 * j + 4, i], pv[:, :, i], vecs[:, 69 * l + 4 * j:69 * l + 4 * j + 4], ALU.add,
                   [("ps", b), ("vecs",)], [("mod", l, j, i)])
            W.release(slot)

        def derive_a(l, which):
            av = a1v if which == 1 else a2v
            sc0 = 8 if which == 1 else 32
            gcol = 48 if which == 1 else 56
            for i in range(2):
                stt(av[:, l, :, i], mod4[:, l, sc0:sc0 + 8, i], 1.0, vecs[:, 69 * l + gcol:69 * l + gcol + 8],
                    ALU.add, ALU.mult,
                    [("mod", l, sc0 // 4, i), ("mod", l, sc0 // 4 + 1, i), ("vecs",)], [("a", l, which, i)])

        class NormEmitter:
            def __init__(self, l, which, last):
                self.l, self.which = l, which
                self.av = a1v if which == 1 else a2v
                self.sh0 = 0 if which == 1 else 24
                self.sq3 = Rb(20480, 4096).rearrange("p (k n) -> p k n", k=8)
                self.rs = Rf(24576, 512)
                self.tmps = [Rf(25600, 512), Rf(26624, 512)]
                self.blks = [(bi, s, n) for bi, (s, n) in enumerate(BLOCKS) if not (which == 2 and last and bi == 0)]
                self.idx_of = {bi: i for i, (bi, s, n) in enumerate(self.blks)}

            def square(self, bi, s, n):
                act(self.sq3[:, :, 0:n], xT3[:, :, s:s + n], AF.Square, [("xT", k, bi) for k in range(KC)], [("nsq",)],
                    scale=1.0 / 32.0)

            def start(self):
                if getattr(self, "started", False):
                    return
                self.started = True
                self.emitted = 0
                self.square(*self.blks[0])

            def ensure(self, upto):
                upto = min(upto, len(self.blks) - 1)
                while self.emitted <= upto:
                    self.block(self.emitted)
                    self.emitted += 1

            def block_now(self, bi):
                if bi in self.idx_of:
                    self.ensure(self.idx_of[bi])

            def block_bi(self, bi):
                if bi in self.idx_of:
                    self.ensure(self.idx_of[bi] + 1)

            def block(self, idx):
                l, which, av, sh0, sq3, rs, tmps = self.l, self.which, self.av, self.sh0, self.sq3, self.rs, self.tmps
                bi, s, n = self.blks[idx]
                i = 1 if bi == 0 else 0
                b = nb()
                for k in range(KC):
                    mm(ps(b, n), ones_m, sq3[:, k, 0:n], k == 0, k == KC - 1, [("nsq",), ("cmat",)], [("ps", b)])
                if idx + 1 < len(self.blks):
                    self.square(*self.blks[idx + 1])
                rsqrt_from_psum(rs[:, 0:n], ps(b, n), [("ps", b), ("eps",)], ("nrs",))
                for k in range(KC):
                    tm = tmps[k % 2]
                    tt(tm[:, 0:n], xT3[:, k, s:s + n], rs[:, 0:n], ALU.mult, [("xT", k, bi), ("nrs",)], [("ntmp", k % 2)])
                    rk = [("ntmp", k % 2), ("a", l, which, i), ("mod", l, (sh0 + k) // 4, i)]
                    if k % 2 == 0:
                        act(hT3[:, k, s:s + n], tm[:, 0:n], AF.Identity, rk, [("hT", k, bi)],
                            scale=av[:, l, k, i:i + 1], bias=mod4[:, l, sh0 + k, i:i + 1])
                    else:
                        S.op("dve", (lambda k, tm, s, n, i: lambda e: e.tensor_scalar(
                            out=hT3[:, k, s:s + n], in0=tm[:, 0:n], scalar1=av[:, l, k, i:i + 1],
                            scalar2=mod4[:, l, sh0 + k, i:i + 1], op0=ALU.mult, op1=ALU.add))(k, tm, s, n, i),
                            rk, [("hT", k, bi)])

        def resid(l, gate0, c, bi, s, n, b, i):
            stt(xT3[:, c, s:s + n], ps(b, n), mod4[:, l, gate0 + c, i:i + 1], xT3[:, c, s:s + n], ALU.mult, ALU.add,
                [("ps", b), ("mod", l, (gate0 + c) // 4, i), ("xT", c, bi)], [("xT", c, bi)])

        def fourier(l, last, ne):
            ftok3 = Rb(0, 9216).rearrange("p (t c) -> p t c", t=18)
            ycs = [Rb(9216, 2048).rearrange("p (g n) -> p g n", g=4), Rb(11264, 2048).rearrange("p (g n) -> p g n", g=4)]
            four3 = Rb(13312, 2048).rearrange("p (g n) -> p g n", g=4)
            AB = Rb(15360, 1536).rearrange("p (g t d) -> p g t d", g=4, t=3)
            dftc = Rb(16896, 1024).rearrange("p (t k n) -> p t k n", t=2, k=2)
            wfv = Rb(17920, 512).rearrange("p (g d) -> p g d", g=4)
            S.op("pool", lambda e: e.dma_start(out=wfv, in_=w_f_d[l].rearrange("g c d -> c g d")), [], [("wfv",)] + [("uT", 7, bi) for bi in range(5)],
                 stream=("f", 0))
            if not last:
                S.op("sp", lambda e: e.dma_start(out=Rb(16896, 1024), in_=dftC_d[:, :]), [], [("dftc",)] + [("uT", 7, bi) for bi in range(5)], stream=("f", 1))
            slot = W.request(wload(lambda: w_in_d[l, :, 1536:2048].rearrange("(k p) c -> p k c", p=128), (8, 512)))
            s3 = wslot(slot).rearrange("p (k c) -> p k c", k=8)
            t0 = 2 if last else 0
            ne.start()
            for idx, (bi, s_, n_) in enumerate(ne.blks):
                ne.ensure(idx + 1)
                for t in range(t0, 18):
                    if blk_of_tile(t) != bi:
                        continue
                    b = nb()
                    for k in range(KC):
                        mm(ps(b), hT3[:, k, t * 128:(t + 1) * 128], s3[:, k, :], k == 0, k == KC - 1,
                           [("hT", k, blk_of_tile(t)), ("wb", slot)], [("ps", b)])
                    evac(ftok3[:, t, :], ps(b), [("ps", b)], [("ftok", t)])
            W.release(slot)
            for g in range(4):
                b = nb()
                for t_, m_ in enumerate((cc_m, scn_m, scp_m)):
                    mm(PS[:, b, t_ * 128:(t_ + 1) * 128], m_, wfv[:, g, :], True, True, [("cmat",), ("wfv",)], [("ps", b)])
                evac(AB[:, g, :, :], PS[:, b, 0:384].rearrange("p (t d) -> p t d", t=3), [("ps", b)], [("AB", g)])
            oslot = W.request(wload(lambda: w_out_d[l, 512:1024, :].rearrange("(k p) c -> p k c", p=128), (4, 1024)))
            o3 = wslot(oslot).rearrange("p (k c) -> p k c", k=4)

            def finish(bi, s, n, i, mirror=False):
                bsel = 2 if mirror else 1
                for g in range(4):
                    b = g
                    mm(ps(b, n), AB[:, g, 0, :], ycs[0][:, g, 0:n], True, False, [("AB", g), ("yc", 0, g)], [("ps", b)])
                    mm(ps(b, n), AB[:, g, bsel, :], ycs[1][:, g, 0:n], False, True, [("AB", g), ("yc", 1, g)], [("ps", b)])
                    evac(four3[:, g, 0:n], ps(b, n), [("ps", b)], [("four", g)])
                for c in range(KC):
                    b = 4 + (c % 4)
                    for g in range(4):
                        mm(ps(b, n), o3[:, g, c * 128:(c + 1) * 128], four3[:, g, 0:n], g == 0, g == 3,
                           [("wb", oslot), ("four", g)], [("ps", b)])
                    resid(l, 16, c, bi, s, n, b, i)

            if not last:
                for trig in range(2):
                    for g in range(4):
                        b = trig * 4 + g
                        for lc in range(2):
                            mm(ps(b, 256), ftok3[:, lc, g * 128:(g + 1) * 128], dftc[:, trig, lc, :], lc == 0, lc == 1,
                               [("ftok", lc), ("dftc",)], [("ps", b)])
                        evac(ycs[trig][:, g, 0:256], ps(b, 256), [("ps", b)], [("yc", trig, g)])
                finish(0, 0, 256, 1)
            for bq in range(2):
                s = LC + bq * 512
                for trig in range(2):
                    for p in range(2):
                        dslot = W.request(wload(
                            (lambda trig, p, bq: lambda: dftL_d[trig, p * 1024:(p + 1) * 1024, bq * 512:(bq + 1) * 512]
                             .rearrange("(k q) c -> q k c", q=128))(trig, p, bq), (8, 512), cast=False))
                        d3 = wslot(dslot).rearrange("p (k c) -> p k c", k=8)
                        for g in range(4):
                            b = trig * 4 + g
                            for lc in range(8):
                                t = 2 + 8 * p + lc
                                mm(ps(b), ftok3[:, t, g * 128:(g + 1) * 128], d3[:, lc, :], p == 0 and lc == 0,
                                   p == 1 and lc == 7, [("ftok", t), ("wb", dslot)], [("ps", b)])
                        W.release(dslot)
                    for g in range(4):
                        b = trig * 4 + g
                        evac(ycs[trig][:, g, :], ps(b), [("ps", b)], [("yc", trig, g)])
                finish(1 + bq, s, 512, 0)
                if bq == 0:
                    bn_ = nb()
                    for g in range(4):
                        for t in range(2, 18):
                            mm(PS[:, bn_, g:g + 1], ftok3[:, t, g * 128:(g + 1) * 128], sgn_m[:, t - 2:t - 1], t == 2, t == 17,
                               [("ftok", t), ("cmat",)], [("ps", bn_)])
                    dcopy(ycs[0][:, :, 0:1], PS[:, bn_, 0:4].rearrange("p (g o) -> p g o", o=1), [("ps", bn_)],
                          [("yc", 0, g) for g in range(4)])
                    S.op("dve", lambda e: e.memset(ycs[1][:, :, 0:1], 0.0), [], [("yc", 1, g) for g in range(4)])
                finish(3 + bq, LC + 1024 + bq * 512, 512, 0, True)
            W.release(oslot)

        def attn_pass(l, hp, last, pending, unit_hook=None):
            qT3 = Rb(0, 4608).rearrange("p (c t) -> p c t", c=2)
            kT3 = Rb(4608, 4608).rearrange("p (c t) -> p c t", c=2)
            V3 = Rb(9216, 4608).rearrange("p (t c) -> p t c", t=18)
            attnA3 = Rb(13824, 4608).rearrange("p (c t) -> p c t", c=2)
            COSv = Rf(18432, 2048)
            SINv = Rf(22528, 2048)
            sqv = Rb(26624, 512)
            rstdv = Rf(27136, 512)
            t1v = Rf(28160, 512)
            t2v = Rf(29184, 512)
            PT = [Rb(18432, 1024).rearrange("p (m n) -> p m n", m=2), Rb(19456, 1024).rearrange("p (m n) -> p m n", m=2),
                  Rb(24576, 1024).rearrange("p (m n) -> p m n", m=2)]
            r1v = Rf(20480, 512)
            r2v = Rf(21504, 512)
            rr3 = Rf(20480, 1024).rearrange("p (m n) -> p m n", m=2)
            u1v = Rf(22528, 512)
            u2v = Rf(23552, 512)
            osq = Rb(24576, 512)
            orstd = Rf(25088, 512)

            S.op("sp", lambda e: e.dma_start(out=COSv, in_=cossin_d[0]), list(bar["keys"]), [("cos",)], stream=("cs", 0))
            S.op("sp", lambda e: e.dma_start(out=SINv, in_=cossin_d[1]), list(bar["keys"]), [("sin",)], stream=("cs", 1))

            slot = W.request(wload(lambda: w_in_d[l, :, 1024 + hp * 256:1024 + (hp + 1) * 256].rearrange("(k p) c -> p k c", p=128), (8, 256)))
            s3 = wslot(slot)[:, 0:2048].rearrange("p (k c) -> p k c", k=8)
            for t in range(18):
                b = nb()
                for k in range(KC):
                    mm(ps(b, 256), hT3[:, k, t * 128:(t + 1) * 128], s3[:, k, :], k == 0, k == KC - 1,
                       [("hT", k, blk_of_tile(t)), ("wb", slot)], [("ps", b)])
                evac(V3[:, t, :], ps(b, 256), [("ps", b)], [("V", t)])
            W.release(slot)
            if stop_after == "attn%d_v" % hp:
                raise StopBuild()

            pend = {"q": list(pending), "stage_b": None}
            for which in ("q", "k"):
                base = 0 if which == "q" else 512
                g1c = 64 if which == "q" else 66
                dst3 = qT3 if which == "q" else kT3
                slot = W.request(wload((lambda base: lambda: w_in_d[l, :, base + hp * 256:base + (hp + 1) * 256]
                                        .rearrange("(k p) c -> p k c", p=128))(base), (8, 256)))
                wq = wslot(slot)[:, 0:2048]
                wsw = wslot(slot)[:, 2048:4096]
                w5 = wq.rearrange("p (k a two s) -> p k a two s", k=8, a=8, two=2, s=16)
                sw5 = wsw.rearrange("p (k a two s) -> p k a two s", k=8, a=8, two=2, s=16)
                s3 = wq.rearrange("p (k c) -> p k c", k=8)
                sw3 = wsw.rearrange("p (k c) -> p k c", k=8)
                ulist = [(cl, bi, s, n) for cl in range(2) for bi, (s, n) in enumerate(BLOCKS)
                         if not (which == "q" and bi == 0 and last)]
                ust = {}

                def stage_sq(ui):
                    cl, bi, s, n = ulist[ui]
                    act(sqv[:, 0:n], ps(ust[ui]["bz"], n), AF.Square, [("ps", ust[ui]["bz"])], [("qsq",)], scale=0.125)

                def stage_a(ui):
                    cl, bi, s, n = ulist[ui]
                    if pend["stage_b"] is not None:
                        subln_b(l, pend["stage_b"])
                        pend["stage_b"] = None
                    if pend["q"]:
                        u_ = pend["q"].pop(0)
                        subln_a(l, u_)
                        pend["stage_b"] = u_
                    if ui >= 1:
                        stage_sq(ui - 1)
                    bz = nb()
                    ust[ui] = {"bz": bz}
                    for k in range(KC):
                        mm(ps(bz, n), s3[:, k, cl * 128:(cl + 1) * 128], hT3[:, k, s:s + n], k == 0, k == KC - 1,
                           [("wb", slot), ("hT", k, bi)], [("ps", bz)])
                    if bi > 0:
                        evac(ZB[:, (ui % 2) * 512:(ui % 2) * 512 + n], ps(bz, n), [("ps", bz)], [("zb", ui % 2)])

                def stage_b(ui):
                    cl, bi, s, n = ulist[ui]
                    bz = ust[ui]["bz"]
                    if bi > 0:
                        bs = nb()
                        mm(ps(bs, n), pm_m, ZB[:, (ui % 2) * 512:(ui % 2) * 512 + n], True, True,
                           [("zb", ui % 2), ("cmat",)], [("ps", bs)])
                    bss = nb()
                    mm(ps(bss, n), bones_m, sqv[:, 0:n], True, True, [("qsq",), ("cmat",)], [("ps", bss)])
                    rsqrt_from_psum(rstdv[:, 0:n], ps(bss, n), [("ps", bss), ("eps",)], ("qrs",))
                    okey = (which + "T", cl, bi)
                    if bi == 0:
                        stt(dst3[:, cl, s:s + n], ps(bz, n), vcol(l, g1c), rstdv[:, 0:n], ALU.mult, ALU.mult,
                            [("ps", bz), ("vecs",), ("qrs",)], [okey])
                    else:
                        stt(t1v, ps(bz, n), vcol(l, g1c), COSv[:, s - LC:s - LC + n], ALU.mult, ALU.mult,
                            [("ps", bz), ("vecs",), ("cos",)], [("qt1",)])
                        stt(t2v, ps(bs, n), vcol(l, g1c + 1), SINv[:, s - LC:s - LC + n], ALU.mult, ALU.mult,
                            [("ps", bs), ("vecs",), ("sin",)], [("qt2",)])
                        tt(t1v, t1v, t2v, ALU.add, [("qt1",), ("qt2",)], [("qt1",)])
                        tt(dst3[:, cl, s:s + n], t1v, rstdv[:, 0:n], ALU.mult, [("qt1",), ("qrs",)], [okey])
                    if unit_hook is not None:
                        unit_hook()

                nu = len(ulist)
                for ui in range(nu):
                    stage_a(ui)
                    if ui >= 1:
                        stage_b(ui - 1)
                stage_sq(nu - 1)
                stage_b(nu - 1)
                for tw in range(2):
                    pass
                W.release(slot)

            while pend["stage_b"] is not None or pend["q"]:
                if pend["stage_b"] is not None:
                    subln_b(l, pend["stage_b"])
                    pend["stage_b"] = None
                if pend["q"]:
                    u_ = pend["q"].pop(0)
                    subln_a(l, u_)
                    pend["stage_b"] = u_
            if stop_after == "attn%d_p3" % hp:
                raise StopBuild()
            barrier()
            steps = []
            units = []
            for cl in range(2):
                groups = []
                if not last:
                    groups.append((0, 0, 256, [0, 1]))
                for bi in range(1, 5):
                    groups.append((bi, BLOCKS[bi][0], 512, list(range(18))))
                for (bi, s, n, tiles) in groups:
                    for j, kt in enumerate(tiles):
                        steps.append(dict(cl=cl, bi=bi, s=s, n=n, kt=kt, first=(j == 0), last=(j == len(tiles) - 1)))
            cnt = {"p": 0}

            def kT_keys(cl, kt):
                return [("kT", cl, blk_of_tile(kt))]

            def emitS(st):
                p = cnt["p"] % 2
                st["pt"] = cnt["p"] % 3
                cnt["p"] += 1
                st["p"] = p
                cl, s, n, kt = st["cl"], st["s"], st["n"], st["kt"]
                for m in range(2):
                    b = 2 * p + m
                    mm(ps(b, n), kT3[64 * m:64 * m + 64, cl, kt * 128:(kt + 1) * 128], qT3[64 * m:64 * m + 64, cl, s:s + n],
                       True, True, kT_keys(cl, kt) + [("qT", cl, st["bi"])], [("ps", b)])
                pt = st["pt"]
                act(PT[pt][:, :, 0:n], PS[:, 2 * p:2 * p + 2, 0:n], AF.Exp, [("ps", 2 * p), ("ps", 2 * p + 1)],
                    [("PT", pt)], scale=0.125)

            def emitPV(st):
                p, cl, n, kt = st["pt"], st["cl"], st["n"], st["kt"]
                for m in range(2):
                    mm(ps(4 + m, n), V3[:, kt, cl * 128:(cl + 1) * 128], PT[p][:, m, 0:n], st["first"], st["last"],
                       [("V", kt), ("PT", p)], [("ps", 4 + m)])
                for m in range(2):
                    mm(ps(6 + m, n), ones_m, PT[p][:, m, 0:n], st["first"], st["last"], [("cmat",), ("PT", p)], [("ps", 6 + m)])

            def emitPost(st):
                cl, bi, s, n = st["cl"], st["bi"], st["s"], st["n"]
                dcopy(u1v[:, 0:n], ps(4, n), [("ps", 4)], [("u1",)])
                dcopy(u2v[:, 0:n], ps(5, n), [("ps", 5)], [("u2",)])
                act(rr3[:, :, 0:n], PS[:, 6:8, 0:n], AF.Ln, [("ps", 6), ("ps", 7)], [("r12",)])
                act(rr3[:, :, 0:n], rr3[:, :, 0:n], AF.Exp, [("r12",)], [("r12",)], scale=-1.0)
                tt(u1v[:, 0:n], u1v[:, 0:n], r1v[:, 0:n], ALU.mult, [("u1",), ("r12",)], [("u1",)])
                tt(u2v[:, 0:n], u2v[:, 0:n], r2v[:, 0:n], ALU.mult, [("u2",), ("r12",)], [("u2",)])
                if hp == 0:
                    dst = attnA3[:, cl, s:s + n]
                    okey = ("attnA", cl, bi)
                else:
                    dst = hT3[:, 2 + cl, s:s + n]
                    okey = ("hT", 2 + cl, bi)
                stt(dst, u2v[:, 0:n], neglam[:, l:l + 1], u1v[:, 0:n], ALU.mult, ALU.add,
                    [("u1",), ("u2",), ("neglam", l)], [okey])
                units.append((dst, okey, n))

            ns = len(steps)
            for j in range(min(2, ns)):
                emitS(steps[j])
            for j in range(ns):
                if j + 2 < ns:
                    emitS(steps[j + 2])
                emitPV(steps[j])
                if steps[j]["last"]:
                    emitPost(steps[j])
            return units

        sqb = SUB[:, 0:512]
        rsb = SUB[:, 512:1536].bitcast(F32)

        def subln_a(l, u):
            dst, okey, n = u
            act(sqb[:, 0:n], dst, AF.Square, [okey], [("sqb",)], scale=1.0 / math.sqrt(128.0))

        def subln_b(l, u):
            dst, okey, n = u
            b = nb()
            mm(ps(b, n), ones_m, sqb[:, 0:n], True, True, [("sqb",), ("cmat",)], [("ps", b)])
            rsqrt_from_psum(rsb[:, 0:n], ps(b, n), [("ps", b), ("eps",)], ("rsb",))
            stt(dst, dst, sgc[:, l:l + 1], rsb[:, 0:n], ALU.mult, ALU.mult, [okey, ("rsb",), ("sgc", l)], [okey])

        def outproj_attn(l, last, pending, hook=None):
            attnA3 = Rb(13824, 4608).rearrange("p (c t) -> p c t", c=2)
            slot = W.request(wload(lambda: w_out_d[l, 0:512, :].rearrange("(k p) c -> p k c", p=128), (4, 1024)))
            o3 = wslot(slot).rearrange("p (k c) -> p k c", k=4)
            byblk = {}
            for u in pending:
                byblk.setdefault(u[1][2], []).append(u)
            oblks = [bi for bi in range(5) if not (last and bi == 0)]

            def do_subln(bi):
                for u in byblk.get(bi, []):
                    subln_a(l, u)
                    subln_b(l, u)

            do_subln(oblks[0])
            for oi, bi in enumerate(oblks):
                s, n = BLOCKS[bi]
                if oi + 1 < len(oblks):
                    do_subln(oblks[oi + 1])
                i = 1 if bi == 0 else 0
                for c in range(KC):
                    b = nb()
                    for j in range(4):
                        if j < 2:
                            rhs = attnA3[:, j, s:s + n]
                            rk = ("attnA", j, bi)
                        else:
                            rhs = hT3[:, j, s:s + n]
                            rk = ("hT", j, bi)
                        mm(ps(b, n), o3[:, j, c * 128:(c + 1) * 128], rhs, j == 0, j == 3, [("wb", slot), rk], [("ps", b)])
                    resid(l, 16, c, bi, s, n, b, i)
                if hook is not None:
                    hook(oi)
            W.release(slot)

        def ffn(l, last, between_parts=None, ne=None, ne_next=None):
            uT3 = Rb(0, 8 * T).rearrange("p (j t) -> p j t", j=8)
            sgt = [Rb(8 * T, 512), Rb(8 * T + 512, 512)]
            parts = [(0, 8), (8, 15), (15, 22)]
            fl = {"i": 0}
            for pi, (j0, j1) in enumerate(parts):
                g0 = j0
                while g0 < j1:
                    g1 = min(g0 + 4, j1)
                    ncol = (g1 - g0) * 128
                    gslot = W.request(wload((lambda g0, g1: lambda: w_gate_d[l, :, g0 * 128:g1 * 128]
                                             .rearrange("(k p) c -> p k c", p=128))(g0, g1), (8, ncol)))
                    uslot = W.request(wload((lambda g0, g1: lambda: w_up_d[l, :, g0 * 128:g1 * 128]
                                             .rearrange("(k p) c -> p k c", p=128))(g0, g1), (8, ncol)))
                    gs3 = wslot(gslot)[:, 0:8 * ncol].rearrange("p (k c) -> p k c", k=8)
                    us3 = wslot(uslot)[:, 0:8 * ncol].rearrange("p (k c) -> p k c", k=8)
                    first_grp = (pi == 0 and g0 == 0 and ne is not None)
                    order = []
                    if first_grp:
                        for bi, (s, n) in enumerate(BLOCKS):
                            for j in range(g0, g1):
                                order.append((j, bi, s, n, j == g0))
                    else:
                        for j in range(g0, g1):
                            for bi, (s, n) in enumerate(BLOCKS):
                                order.append((j, bi, s, n, False))
                    for (j, bi, s, n, hook) in order:
                        jc = j - g0
                        if True:
                            if last and bi == 0:
                                continue
                            if hook:
                                ne.block_bi(bi)
                            bg = nb()
                            for k in range(KC):
                                mm(ps(bg, n), gs3[:, k, jc * 128:(jc + 1) * 128], hT3[:, k, s:s + n], k == 0, k == KC - 1,
                                   [("wb", gslot), ("hT", k, bi)], [("ps", bg)])
                            bu = nb()
                            for k in range(KC):
                                mm(ps(bu, n), us3[:, k, jc * 128:(jc + 1) * 128], hT3[:, k, s:s + n], k == 0, k == KC - 1,
                                   [("wb", uslot), ("hT", k, bi)], [("ps", bu)])
                            f = fl["i"]
                            fl["i"] ^= 1
                            act(sgt[f][:, 0:n], ps(bg, n), AF.Silu, [("ps", bg)], [("sgt", f)])
                            tt(uT3[:, j - j0, s:s + n], sgt[f][:, 0:n], ps(bu, n), ALU.mult, [("sgt", f), ("ps", bu)],
                               [("uT", j - j0, bi)])
                    W.release(gslot)
                    W.release(uslot)
                    g0 = g1
                dsl = []
                d0 = j0
                while d0 < j1:
                    d1 = min(d0 + 4, j1)
                    dslot = W.request(wload((lambda d0, d1: lambda: w_down_d[l, d0 * 128:d1 * 128, :]
                                             .rearrange("(k p) c -> p k c", p=128))(d0, d1), (d1 - d0, 1024)))
                    dsl.append((d0, d1, dslot, wslot(dslot)[:, 0:(d1 - d0) * 1024].rearrange("p (k c) -> p k c", k=d1 - d0)))
                    d0 = d1
                for bi, (s, n) in enumerate(BLOCKS):
                    if last and bi == 0:
                        continue
                    i = 1 if bi == 0 else 0
                    for c in range(KC):
                        b = nb()
                        for (d0, d1, dslot, d3) in dsl:
                            for j in range(d0, d1):
                                mm(ps(b, n), d3[:, j - d0, c * 128:(c + 1) * 128], uT3[:, j - j0, s:s + n], j == j0, j == j1 - 1,
                                   [("wb", dslot), ("uT", j - j0, bi)], [("ps", b)])
                        resid(l, 40, c, bi, s, n, b, i)
                    if pi == len(parts) - 1 and ne_next is not None:
                        ne_next.start()
                        ne_next.block_now(bi)
                    if pi == len(parts) - 1 and last and bi > 0:
                        S.op("sp", (lambda s, bi: lambda e: e.dma_start(
                            out=yT_d.rearrange("(k p) t -> p k t", p=128)[:, :, s - LC:s - LC + 512],
                            in_=xT3[:, :, s:s + 512]))(s, bi),
                            [("xT", k, bi) for k in range(KC)], [("y", bi)], stream=("y", bi))
                        out_keys.append(("y", bi))
                for (d0, d1, dslot, d3) in dsl:
                    W.release(dslot)
                if between_parts is not None:
                    between_parts(pi)

        out_keys = []

        def stop(name):
            return stop_after == name

        done = False
        for l in range(DEPTH):
          try:
                last = l == DEPTH - 1
                if l == 0:
                    for j in range(4):
                        ada_piece(0, j)
                if l == 0:
                    derive_a(l, 1)
                    for j in range(4, 6):
                        ada_piece(0, j)
                    fourier(l, last, NormEmitter(l, 1, last))
                else:
                    fourier(l, last, ne1_next)
                if stop("fourier"):
                    done = True
                    break
                pending = []
                adaq = {"q": [], "n": 0}
                if l == 0:
                    adaq["q"] = [(0, j) for j in range(6, 12)] + [(1, j) for j in range(12)]

                def ada_pop(k_=1):
                    for _ in range(k_):
                        if adaq["q"]:
                            ll, jj = adaq["q"].pop(0)
                            ada_piece(ll, jj)
                            if (ll, jj) == (1, 3):
                                derive_a(1, 1)

                def unit_hook():
                    adaq["n"] += 1
                    if adaq["n"] % 3 == 0:
                        ada_pop()

                for hp in range(2):
                    barrier()
                    pending = attn_pass(l, hp, last, pending, unit_hook if l == 0 else None)
                    if stop("attn%d" % hp):
                        done = True
                        break
                if done:
                    break
                if l == 0:
                    def ohook(oi):
                        left = len(adaq["q"])
                        ada_pop((left + (4 - oi)) // (5 - oi) if oi < 4 else left)
                    while adaq["q"] and adaq["q"][0][0] == 0:
                        ada_pop()
                    outproj_attn(l, last, pending, ohook)
                else:
                    outproj_attn(l, last, pending)
                if stop("outproj"):
                    done = True
                    break
                derive_a(l, 2)
                ne2 = NormEmitter(l, 2, last)
                ne2.start()
                if l == 0:
                    ne1_next = NormEmitter(1, 1, True)

                    ffn(l, last, None, ne2, ne1_next)
                else:
                    ffn(l, last, None, ne2)
                if stop("layer0"):
                    done = True
                    break
          except StopBuild:
            break

        okeys = list(out_keys)
        if not okeys:
            for k in range(KC):
                S.op("sp", (lambda k: lambda e: e.dma_start(out=yT_d[k * 128:(k + 1) * 128, :], in_=xT3[:, k, LC:T]))(k),
                     [("xT", k, bi) for bi in range(1, 5)], [("y", k)], stream=("y", k))
                okeys.append(("y", k))
        if debug:
            barrier()
            for name, (apfn, keys, shape, dt) in debug.items():
                dd = dbg_d[name]
                S.op("sp", (lambda apfn, dd: lambda e: e.dma_start(out=dd, in_=apfn(locals_)))(apfn, dd), list(bar["keys"]), [("dbg", name)],
                     stream=("dbg", name))
                okeys.append(("dbg", name))
        S.op("sp", lambda e: None, okeys, [])

    EPS_AP = SM[:, 430:431]
    dbg_d = {}
    if debug:
        for name, (apfn, keys, shape, dt) in debug.items():
            dbg_d[name] = nc.dram_tensor("dbg_" + name, list(shape), dt, kind="ExternalOutput").ap()
    locals_ = dict(xT3=xT3, hT3=hT3, R=R, SM=SM, mod4=mod4, a1v=a1v, Rb=Rb, Rf=Rf)

    state["bank"] = 0
    rec = WBM(Sched(), None)
    build(Sched(), rec)
    state["bank"] = 0
    S = Sched()
    W = WBM(S, rec.rec)
    build(S, W)
    with nc.Block() as block:
        S.emit(nc, block)
    return nc


def _perm64():
    d = np.arange(64)
    return np.where((d % 32) < 16, d + 16, d - 16)


def _tok_of_pos():
    tp = np.arange(L)
    m = np.arange(1, L // 2)
    tp[L // 2 + m] = L - m
    return tp


def _host_consts():
    inv = (10000.0 ** (-np.arange(0, 32, 2, dtype=np.float32) / np.float32(32))).astype(np.float32)
    tok = np.arange(L)
    row = (tok // 64).astype(np.float32)
    col = (tok % 64).astype(np.float32)
    cos = np.zeros((128, L), np.float32)
    sin = np.zeros((128, L), np.float32)
    for p in range(128):
        d = p % 64
        pos = row if d < 32 else col
        ang = (pos * inv[d % 16]).astype(np.float32)
        sign = -1.0 if (d % 32) < 16 else 1.0
        cos[p] = np.cos(ang)
        sin[p] = sign * np.sin(ang)
    tp = _tok_of_pos()
    cossin = np.stack([cos[:, tp], sin[:, tp]]).astype(np.float32)

    def dft(n):
        a = np.arange(n, dtype=np.int64)
        m = (np.outer(a, a) % n).astype(np.float64) * (2.0 * np.pi / n)
        return np.cos(m) / np.sqrt(n), np.sin(m) / np.sqrt(n)

    cl, sl = dft(L)
    dftL = np.stack([cl[tp][:, :L // 2], sl[tp][:, :L // 2]]).astype(ml_dtypes.bfloat16)
    sgn = (np.where(tp % 2 == 0, 1.0, -1.0) / np.sqrt(L)).reshape(16, 128).T
    cc_, sc_ = dft(LC)
    dc = np.stack([cc_, sc_])
    dftC = dc.reshape(2, 2, 128, 256).transpose(2, 0, 1, 3).reshape(128, 1024).astype(ml_dtypes.bfloat16)
    c128, s128 = dft(128)
    ones = np.ones((128, 128))
    bones = np.zeros((128, 128))
    bones[:64, :64] = 1
    bones[64:, 64:] = 1
    p64 = _perm64()
    partner = np.concatenate([p64, 64 + p64])
    pm = np.zeros((128, 128))
    pm[partner, np.arange(128)] = 1.0
    cmat = np.concatenate([ones, bones, c128, -s128, s128, sgn, pm], axis=1).astype(ml_dtypes.bfloat16)
    return cossin, dftL, dftC, cmat


def _prep_inputs(inp):
    f32 = np.float32
    cossin, dftL, dftC, cmat = _host_consts()
    perm = _perm64()
    tp = _tok_of_pos()
    shared = {
        "cossin": cossin, "dftL": dftL, "dftC": dftC, "cmat": cmat,
        "w_ada": np.ascontiguousarray(inp["w_ada"], f32), "w_in": np.ascontiguousarray(inp["w_in"], f32),
        "w_out": np.ascontiguousarray(inp["w_out"], f32), "w_gate": np.ascontiguousarray(inp["w_gate"], f32),
        "w_up": np.ascontiguousarray(inp["w_up"], f32), "w_down": np.ascontiguousarray(inp["w_down"], f32),
        "w_fourier": np.ascontiguousarray(inp["w_fourier"], f32),
    }
    lamin = np.zeros((128, 2, 4, 64), f32)
    for l in range(DEPTH):
        for j, nm in enumerate(["lambda_q1", "lambda_k1", "lambda_q2", "lambda_k2"]):
            lamin[:, l, j, :] = np.asarray(inp[nm], f32)[l][None, :]
    shared["lamin"] = lamin.reshape(128, 512)
    maps = []
    for b in range(8):
        vecs = np.zeros((128, NV), f32)
        for l in range(DEPTH):
            o = 69 * l
            vecs[:, o:o + 48] = np.asarray(inp["b_ada"], f32)[l].reshape(48, 128).T
            vecs[:, o + 48:o + 56] = np.asarray(inp["norm1_g"], f32)[l].reshape(8, 128).T
            vecs[:, o + 56:o + 64] = np.asarray(inp["norm2_g"], f32)[l].reshape(8, 128).T
            qg = np.asarray(inp["q_norm_g"], f32)[l]
            kg = np.asarray(inp["k_norm_g"], f32)[l]
            vecs[:, o + 64] = np.tile(qg, 2)
            vecs[:, o + 65] = np.tile(qg[perm], 2)
            vecs[:, o + 66] = np.tile(kg, 2)
            vecs[:, o + 67] = np.tile(kg[perm], 2)
            vecs[:, o + 68] = np.asarray(inp["subln_g"], f32)[l]
        vecs[:, 138:146] = np.asarray(inp["c"], f32)[b].reshape(8, 128).T
        vecs[:, 146:154] = np.asarray(inp["c_ctx"], f32).reshape(8, 128).T
        m = dict(shared)
        m["xT"] = np.ascontiguousarray(np.asarray(inp["x"], f32)[b][tp].T)
        m["cT"] = np.ascontiguousarray(np.asarray(inp["ctx"], f32)[b].T)
        m["vecs"] = vecs
        maps.append(m)
    return maps


_NC_CACHE = {}


def kernel(**inputs):
    maps = _prep_inputs(inputs)
    if "nc" not in _NC_CACHE:
        _NC_CACHE["nc"] = build_program()
    nc = _NC_CACHE["nc"]
    res = run_bass_kernel_spmd(nc, maps, core_ids=list(range(8)))
    tp = _tok_of_pos()
    out = np.empty((8, L, D), np.float32)
    for b, r in enumerate(res.results):
        out[b][tp] = r["yT"].T
    return out
```

```python
import math
import numpy as np
import ml_dtypes
import concourse.bass as bass
import concourse.mybir as mybir
from concourse.bass_utils import run_bass_kernel_spmd

F32 = mybir.dt.float32
BF16 = mybir.dt.bfloat16
AF = mybir.ActivationFunctionType
ALU = mybir.AluOpType

D = 1024
L = 2048
LC = 256
T = L + LC
KC = 8
DFF = 2816
NFF = 22
DEPTH = 2
EPS = 1e-6
BLOCKS = [(0, 256), (256, 512), (768, 512), (1280, 512), (1792, 512)]
NV = 154
NSLOT = 4
LOOKAHEAD = 3
EPOCH = 4000


def lambda_init(i):
    return 0.8 - 0.6 * math.exp(-0.3 * i)


def blk_of_tile(t):
    return 0 if t < 2 else 1 + (t - 2) // 4


class StopBuild(Exception):
    pass


class Op:
    __slots__ = ("eng", "fn", "deps", "signal", "count", "is_dma", "stream")


class Sched:
    ENGS = ["pe", "act", "dve", "pool", "sp"]

    def __init__(self):
        self.ops = {e: [] for e in self.ENGS}
        self.lastw = {}
        self.readers = {}
        self.dma_count = {}

    def op(self, eng, fn, r=(), w=(), stream=None):
        o = Op()
        o.eng = eng
        o.fn = fn
        o.signal = False
        o.count = 0
        o.is_dma = stream is not None
        o.stream = stream
        deps = {}

        def add(d):
            if d is o:
                return
            if (not d.is_dma) and (not o.is_dma) and d.eng == "pe" and eng == "pe":
                return
            deps[id(d)] = d

        for k in r:
            d = self.lastw.get(k)
            if d is not None:
                add(d)
            if k[0] == "ps":
                rd = self.readers.get(k)
                if rd:
                    for e2, d2 in rd[0].items():
                        if e2 != eng:
                            add(d2)
        for k in w:
            d = self.lastw.get(k)
            if d is not None:
                add(d)
            rd = self.readers.get(k)
            if rd:
                for d in rd[0].values():
                    add(d)
                for d in rd[1]:
                    add(d)
        for k in r:
            rd = self.readers.setdefault(k, ({}, []))
            if o.is_dma:
                rd[1].append(o)
            else:
                rd[0][eng] = o
        for k in w:
            self.lastw[k] = o
            self.readers[k] = ({}, [])
        o.deps = list(deps.values())
        for d in o.deps:
            d.signal = True
        if o.is_dma:
            c = self.dma_count.get(stream, 0) + 16
            self.dma_count[stream] = c
            o.count = c
        self.ops[eng].append(o)
        return o

    def emit(self, nc, block):
        esems = {}
        dsems = {}
        for e in self.ENGS:
            c = 0
            for o in self.ops[e]:
                if (not o.is_dma) and o.signal:
                    c += 1
                    o.count = c
            nep = (c + EPOCH - 1) // EPOCH + 1
            esems[e] = [nc.alloc_semaphore("se_%s_%d" % (e, i)) for i in range(nep)]
        for i, s in enumerate(self.dma_count.keys()):
            dsems[s] = nc.alloc_semaphore("sd_%d" % i)

        def semval(d):
            if d.is_dma:
                return dsems[d.stream], d.count
            ep = (d.count - 1) // EPOCH
            return esems[d.eng][ep], d.count - ep * EPOCH

        def run(e, h):
            waited = {}
            for o in self.ops[e]:
                for d in o.deps:
                    sem, val = semval(d)
                    key = id(sem)
                    if waited.get(key, 0) < val:
                        h.wait_ge(sem, val)
                        waited[key] = val
                inst = o.fn(h)
                if inst is None:
                    continue
                if o.is_dma:
                    inst.then_inc(dsems[o.stream], 16)
                elif o.signal:
                    sem, _ = semval(o)
                    inst.then_inc(sem, 1)

        block.tensor(lambda h: run("pe", h))
        block.scalar(lambda h: run("act", h))
        block.vector(lambda h: run("dve", h))
        block.gpsimd(lambda h: run("pool", h))
        block.sync(lambda h: run("sp", h))


class WBM:
    def __init__(self, S, future=None):
        self.S = S
        self.future = future
        self.rec = []
        self.pos = 0
        self.issued = 0
        self.free = list(range(NSLOT))
        self.slot_of = {}

    def _issue(self):
        i = self.issued
        slot = self.free.pop(0)
        self.future[i](self.S, slot)
        self.slot_of[i] = slot
        self.issued += 1

    def _pump(self):
        while self.free and self.issued < min(len(self.future), self.pos + LOOKAHEAD):
            self._issue()

    def request(self, loader):
        if self.future is None:
            self.rec.append(loader)
            return 0
        i = self.pos
        self.pos += 1
        while self.issued <= i:
            assert self.free, "no free weight slot"
            self._issue()
        self._pump()
        return self.slot_of[i]

    def release(self, slot):
        if self.future is None:
            return
        self.free.append(slot)
        self._pump()


def build_program(debug=None, stop_after=None):
    nc = bass.Bass("TRN2", target_bir_lowering=False)
    dr = {}

    def din(name, shape, dt):
        dr[name] = nc.dram_tensor(name, list(shape), dt, kind="ExternalInput").ap()
        return dr[name]

    xT_d = din("xT", [D, L], F32)
    cT_d = din("cT", [D, LC], F32)
    vecs_d = din("vecs", [128, NV], F32)
    lamin_d = din("lamin", [128, 512], F32)
    cossin_d = din("cossin", [2, 128, L], F32)
    dftL_d = din("dftL", [2, L, L // 2], BF16)
    dftC_d = din("dftC", [128, 2 * 2 * 256], BF16)
    cmat_d = din("cmat", [128, 784], BF16)
    w_ada_d = din("w_ada", [DEPTH, D, 6 * D], F32)
    w_in_d = din("w_in", [DEPTH, D, 2048], F32)
    w_out_d = din("w_out", [DEPTH, D, D], F32)
    w_gate_d = din("w_gate", [DEPTH, D, DFF], F32)
    w_up_d = din("w_up", [DEPTH, D, DFF], F32)
    w_down_d = din("w_down", [DEPTH, DFF, D], F32)
    w_f_d = din("w_fourier", [DEPTH, 4, 128, 128], F32)
    yT_d = nc.dram_tensor("yT", [D, L], F32, kind="ExternalOutput").ap()

    XT = nc.alloc_sbuf_tensor("XT", [128, KC * T], F32)
    HT = nc.alloc_sbuf_tensor("HT", [128, KC * T], BF16)
    WBt = nc.alloc_sbuf_tensor("WB", [128, NSLOT * 4096], BF16)
    R = nc.alloc_sbuf_tensor("R", [128, 30208], BF16)
    SM = nc.alloc_sbuf_tensor("SM", [128, 512], F32)
    ZB = nc.alloc_sbuf_tensor("ZB", [128, 1024], BF16)
    CM = nc.alloc_sbuf_tensor("CM", [128, 800], BF16)
    SUB = nc.alloc_sbuf_tensor("SUB", [128, 1536], BF16)
    PS = nc.alloc_psum_tensor("PS", [128, 8, 512], F32)

    xT3 = XT[:, :].rearrange("p (k t) -> p k t", k=KC)
    hT3 = HT[:, :].rearrange("p (k t) -> p k t", k=KC)

    def wslot(s):
        return WBt[:, s * 4096:(s + 1) * 4096]

    def Rb(a, n):
        return R[:, a:a + n]

    def Rf(a, n):
        return R[:, a:a + 2 * n].bitcast(F32)

    vecs = SM[:, 0:NV]
    mod4 = SM[:, 160:160 + 192].rearrange("p (l c i) -> p l c i", l=2, c=48, i=2)
    a1v = SM[:, 352:384].rearrange("p (l k i) -> p l k i", l=2, k=8, i=2)
    a2v = SM[:, 384:416].rearrange("p (l k i) -> p l k i", l=2, k=8, i=2)
    neglam = SM[:, 416:418]
    sgc = SM[:, 418:420]
    lam_s = SM[:, 420:428]
    ones_m = CM[:, 0:128]
    bones_m = CM[:, 128:256]
    cc_m = CM[:, 256:384]
    scn_m = CM[:, 384:512]
    scp_m = CM[:, 512:640]
    sgn_m = CM[:, 640:656]
    pm_m = CM[:, 656:784]
    siluc3 = CM[:, 784:800].rearrange("p (k i) -> p k i", k=8, i=2)

    def vcol(l, j):
        return vecs[:, 69 * l + j:69 * l + j + 1]

    def ps(b, n=512):
        return PS[:, b, 0:n]

    state = {"bank": 0}

    def nb():
        b = state["bank"]
        state["bank"] = (b + 1) % 8
        return b

    def build(S, W):
        def mm(out, lhsT, rhs, start, stop, r, w):
            S.op("pe", lambda e: e.matmul(out, lhsT, rhs, start=start, stop=stop), r, w)

        def act(out, in_, func, r, w, scale=1.0, bias=0.0):
            S.op("act", lambda e: e.activation(out=out, in_=in_, func=func, bias=bias, scale=scale), r, w)

        def stt(out, in0, scalar, in1, op0, op1, r, w):
            S.op("dve", lambda e: e.scalar_tensor_tensor(out=out, in0=in0, scalar=scalar, in1=in1, op0=op0, op1=op1), r, w)

        def tt(out, in0, in1, op, r, w):
            S.op("dve", lambda e: e.tensor_tensor(out=out, in0=in0, in1=in1, op=op), r, w)

        def dcopy(out, in_, r, w):
            S.op("dve", lambda e: e.tensor_copy(out=out, in_=in_), r, w)

        evac_flip = {"i": 0}

        def evac(out, in_, r, w):
            evac_flip["i"] ^= 1
            if evac_flip["i"]:
                S.op("act", lambda e: e.activation(out=out, in_=in_, func=AF.Copy), r, w)
            else:
                dcopy(out, in_, r, w)

        bar = {"keys": []}

        def barrier():
            S.op("pe", lambda e: e.matmul(PS[0:1, 7, 511:512], ones_m[:, 0:1], ones_m[:, 0:1], start=True, stop=True),
                 [("cmat",)], [("bar1", "pe"), ("ps", 7)])
            S.op("act", lambda e: e.activation(out=SM[:, 432:433], in_=SM[:, 430:431], func=AF.Copy), [("eps",)], [("bar1", "act")])
            S.op("dve", lambda e: e.tensor_copy(out=SM[:, 433:434], in_=SM[:, 430:431]), [("eps",)], [("bar1", "dve")])
            b1 = [("bar1", "pe"), ("bar1", "act"), ("bar1", "dve")]
            S.op("pe", lambda e: e.matmul(PS[0:1, 7, 511:512], ones_m[:, 0:1], ones_m[:, 0:1], start=True, stop=True),
                 b1 + [("cmat",)], [("bar2", "pe"), ("ps", 7)])
            S.op("act", lambda e: e.activation(out=SM[:, 434:435], in_=SM[:, 430:431], func=AF.Copy), b1 + [("eps",)], [("bar2", "act")])
            S.op("dve", lambda e: e.tensor_copy(out=SM[:, 435:436], in_=SM[:, 430:431]), b1 + [("eps",)], [("bar2", "dve")])
            bar["keys"] = [("bar2", "pe"), ("bar2", "act"), ("bar2", "dve")]

        def rsqrt_from_psum(dst, src, rkeys, wkey):
            act(dst, src, AF.Ln, rkeys, [wkey], scale=1.0, bias=EPS_AP)
            act(dst, dst, AF.Exp, [wkey], [wkey], scale=-0.5)

        def wload(src_ap_fn, shape3, cast=True):
            def loader(S_, slot):
                k, c = shape3
                dst = wslot(slot)[:, 0:k * c].rearrange("p (k c) -> p k c", k=k)
                eng = "pool" if cast else "sp"
                S_.op(eng, lambda e: e.dma_start(out=dst, in_=src_ap_fn()), [],
                      [("wb", slot), ("wbsw", slot)], stream=("wb", slot, eng))
            return loader

        S.op("sp", lambda e: e.dma_start(out=vecs, in_=vecs_d[:, :]), [], [("vecs",)], stream=("c", 0))
        S.op("sp", lambda e: e.dma_start(out=CM[:, 0:784], in_=cmat_d[:, :]), [], [("cmat",)], stream=("c", 1))
        lamv = Rf(20000, 512)
        S.op("sp", lambda e: e.dma_start(out=xT3[:, :, 0:LC], in_=cT_d.rearrange("(k p) t -> p k t", p=128)),
             [], [("xT", k, 0) for k in range(KC)], stream=("xc", 0))
        def load_x_block(bi, rkeys):
            s0 = BLOCKS[bi][0]
            S.op("sp", (lambda bi, s0: lambda e: e.dma_start(
                out=xT3[:, :, s0:s0 + 512],
                in_=xT_d.rearrange("(k p) t -> p k t", p=128)[:, :, s0 - LC:s0 - LC + 512]))(bi, s0),
                rkeys, [("xT", k, bi) for k in range(KC)], stream=("xl", bi))

        load_x_block(1, [])
        S.op("sp", lambda e: e.dma_start(out=lamv, in_=lamin_d[:, :]), [], [("lamin",)], stream=("c", 2))

        S.op("dve", lambda e: e.memset(SM[:, 430:431], EPS), [], [("eps",)])
        for i in range(2):
            act(siluc3[:, :, i], vecs[:, 138 + 8 * i:146 + 8 * i], AF.Silu, [("vecs",)], [("siluc", i)])
        lam4 = lamv.rearrange("p (l f d) -> p l f d", l=2, f=4, d=64)
        tmpl = Rf(21024, 64)
        for l in range(DEPTH):
            for j in range(2):
                tt(tmpl, lam4[:, l, 2 * j, :], lam4[:, l, 2 * j + 1, :], ALU.mult, [("lamin",)], [("tmpl",)])
                S.op("dve", (lambda l, j: lambda e: e.reduce_sum(out=lam_s[:, 2 * l + j:2 * l + j + 1], in_=tmpl,
                                                                axis=mybir.AxisListType.X))(l, j),
                     [("tmpl",)], [("lams", l, j)])
            act(lam_s[:, 4 + 2 * l:6 + 2 * l], lam_s[:, 2 * l:2 * l + 2], AF.Exp,
                [("lams", l, 0), ("lams", l, 1)], [("lame", l)])
            tt(neglam[:, l:l + 1], lam_s[:, 5 + 2 * l:6 + 2 * l], lam_s[:, 4 + 2 * l:5 + 2 * l], ALU.subtract,
               [("lame", l)], [("neglam", l)])
            S.op("dve", (lambda l: lambda e: e.tensor_scalar(out=neglam[:, l:l + 1], in0=neglam[:, l:l + 1],
                                                             scalar1=-lambda_init(l), scalar2=None, op0=ALU.add))(l),
                 [("neglam", l)], [("neglam", l)])
            S.op("dve", (lambda l: lambda e: e.tensor_scalar(out=sgc[:, l:l + 1], in0=vcol(l, 68),
                                                             scalar1=1.0 - lambda_init(l), scalar2=None, op0=ALU.mult))(l),
                 [("vecs",)], [("sgc", l)])

        def ada_piece(l, j):
            slot = W.request(wload(lambda: w_ada_d[l, :, j * 512:(j + 1) * 512].rearrange("(k p) c -> p k c", p=128), (8, 512)))
            s3 = wslot(slot).rearrange("p (k c) -> p k c", k=8)
            b = nb()
            for fc in range(4):
                for k in range(KC):
                    mm(PS[:, b, 2 * fc:2 * fc + 2], s3[:, k, fc * 128:(fc + 1) * 128], siluc3[:, k, :],
                       k == 0, k == KC - 1, [("wb", slot), ("siluc", 0), ("siluc", 1)], [("ps", b)])
            pv = PS[:, b, 0:8].rearrange("p (f i) -> p f i", i=2)
            for i in range(2):
                tt(mod4[:, l, 4 * j:4 * j + 4, i], pv[:, :, i], vecs[:, 69 * l + 4 * j:69 * l + 4 * j + 4], ALU.add,
                   [("ps", b), ("vecs",)], [("mod", l, j, i)])
            W.release(slot)

        def derive_a(l, which):
            av = a1v if which == 1 else a2v
            sc0 = 8 if which == 1 else 32
            gcol = 48 if which == 1 else 56
            for i in range(2):
                stt(av[:, l, :, i], mod4[:, l, sc0:sc0 + 8, i], 1.0, vecs[:, 69 * l + gcol:69 * l + gcol + 8],
                    ALU.add, ALU.mult,
                    [("mod", l, sc0 // 4, i), ("mod", l, sc0 // 4 + 1, i), ("vecs",)], [("a", l, which, i)])

        class NormEmitter:
            def __init__(self, l, which, last):
                self.l, self.which = l, which
                self.av = a1v if which == 1 else a2v
                self.sh0 = 0 if which == 1 else 24
                self.sq3 = Rb(20480, 4096).rearrange("p (k n) -> p k n", k=8)
                self.rs = Rf(24576, 512)
                self.tmps = [Rf(25600, 512), Rf(26624, 512)]
                self.blks = [(bi, s, n) for bi, (s, n) in enumerate(BLOCKS) if not (which == 2 and last and bi == 0)]
                self.idx_of = {bi: i for i, (bi, s, n) in enumerate(self.blks)}

            def square(self, bi, s, n):
                act(self.sq3[:, :, 0:n], xT3[:, :, s:s + n], AF.Square, [("xT", k, bi) for k in range(KC)], [("nsq",)],
                    scale=1.0 / 32.0)

            def start(self):
                if getattr(self, "started", False):
                    return
                self.started = True
                self.emitted = 0
                self.square(*self.blks[0])

            def ensure(self, upto):
                upto = min(upto, len(self.blks) - 1)
                while self.emitted <= upto:
                    self.block(self.emitted)
                    self.emitted += 1

            def block_now(self, bi):
                if bi in self.idx_of:
                    self.ensure(self.idx_of[bi])

            def block_bi(self, bi):
                if bi in self.idx_of:
                    self.ensure(self.idx_of[bi] + 1)

            def block(self, idx):
                l, which, av, sh0, sq3, rs, tmps = self.l, self.which, self.av, self.sh0, self.sq3, self.rs, self.tmps
                bi, s, n = self.blks[idx]
                i = 1 if bi == 0 else 0
                b = nb()
                for k in range(KC):
                    mm(ps(b, n), ones_m, sq3[:, k, 0:n], k == 0, k == KC - 1, [("nsq",), ("cmat",)], [("ps", b)])
                if idx + 1 < len(self.blks):
                    self.square(*self.blks[idx + 1])
                rsqrt_from_psum(rs[:, 0:n], ps(b, n), [("ps", b), ("eps",)], ("nrs",))
                for k in range(KC):
                    tm = tmps[k % 2]
                    tt(tm[:, 0:n], xT3[:, k, s:s + n], rs[:, 0:n], ALU.mult, [("xT", k, bi), ("nrs",)], [("ntmp", k % 2)])
                    rk = [("ntmp", k % 2), ("a", l, which, i), ("mod", l, (sh0 + k) // 4, i)]
                    if k % 2 == 0:
                        act(hT3[:, k, s:s + n], tm[:, 0:n], AF.Identity, rk, [("hT", k, bi)],
                            scale=av[:, l, k, i:i + 1], bias=mod4[:, l, sh0 + k, i:i + 1])
                    else:
                        S.op("dve", (lambda k, tm, s, n, i: lambda e: e.tensor_scalar(
                            out=hT3[:, k, s:s + n], in0=tm[:, 0:n], scalar1=av[:, l, k, i:i + 1],
                            scalar2=mod4[:, l, sh0 + k, i:i + 1], op0=ALU.mult, op1=ALU.add))(k, tm, s, n, i),
                            rk, [("hT", k, bi)])

        def resid(l, gate0, c, bi, s, n, b, i):
            stt(xT3[:, c, s:s + n], ps(b, n), mod4[:, l, gate0 + c, i:i + 1], xT3[:, c, s:s + n], ALU.mult, ALU.add,
                [("ps", b), ("mod", l, (gate0 + c) // 4, i), ("xT", c, bi)], [("xT", c, bi)])

        def fourier(l, last, ne):
            ftok3 = Rb(0, 9216).rearrange("p (t c) -> p t c", t=18)
            ycs = [Rb(9216, 2048).rearrange("p (g n) -> p g n", g=4), Rb(11264, 2048).rearrange("p (g n) -> p g n", g=4)]
            four3 = Rb(13312, 2048).rearrange("p (g n) -> p g n", g=4)
            AB = Rb(15360, 1536).rearrange("p (g t d) -> p g t d", g=4, t=3)
            dftc = Rb(16896, 1024).rearrange("p (t k n) -> p t k n", t=2, k=2)
            wfv = Rb(17920, 512).rearrange("p (g d) -> p g d", g=4)
            S.op("pool", lambda e: e.dma_start(out=wfv, in_=w_f_d[l].rearrange("g c d -> c g d")), [], [("wfv",)] + [("uT", 7, bi) for bi in range(5)],
                 stream=("f", 0))
            if not last:
                S.op("sp", lambda e: e.dma_start(out=Rb(16896, 1024), in_=dftC_d[:, :]), [], [("dftc",)] + [("uT", 7, bi) for bi in range(5)], stream=("f", 1))
            slot = W.request(wload(lambda: w_in_d[l, :, 1536:2048].rearrange("(k p) c -> p k c", p=128), (8, 512)))
            s3 = wslot(slot).rearrange("p (k c) -> p k c", k=8)
            t0 = 2 if last else 0
            ne.start()
            for idx, (bi, s_, n_) in enumerate(ne.blks):
                ne.ensure(idx + 1)
                for t in range(t0, 18):
                    if blk_of_tile(t) != bi:
                        continue
                    b = nb()
                    for k in range(KC):
                        mm(ps(b), hT3[:, k, t * 128:(t + 1) * 128], s3[:, k, :], k == 0, k == KC - 1,
                           [("hT", k, blk_of_tile(t)), ("wb", slot)], [("ps", b)])
                    evac(ftok3[:, t, :], ps(b), [("ps", b)], [("ftok", t)])
            W.release(slot)
            for g in range(4):
                b = nb()
                for t_, m_ in enumerate((cc_m, scn_m, scp_m)):
                    mm(PS[:, b, t_ * 128:(t_ + 1) * 128], m_, wfv[:, g, :], True, True, [("cmat",), ("wfv",)], [("ps", b)])
                evac(AB[:, g, :, :], PS[:, b, 0:384].rearrange("p (t d) -> p t d", t=3), [("ps", b)], [("AB", g)])
            oslot = W.request(wload(lambda: w_out_d[l, 512:1024, :].rearrange("(k p) c -> p k c", p=128), (4, 1024)))
            o3 = wslot(oslot).rearrange("p (k c) -> p k c", k=4)

            def finish(bi, s, n, i, mirror=False):
                bsel = 2 if mirror else 1
                for g in range(4):
                    b = g
                    mm(ps(b, n), AB[:, g, 0, :], ycs[0][:, g, 0:n], True, False, [("AB", g), ("yc", 0, g)], [("ps", b)])
                    mm(ps(b, n), AB[:, g, bsel, :], ycs[1][:, g, 0:n], False, True, [("AB", g), ("yc", 1, g)], [("ps", b)])
                    evac(four3[:, g, 0:n], ps(b, n), [("ps", b)], [("four", g)])
                for c in range(KC):
                    b = 4 + (c % 4)
                    for g in range(4):
                        mm(ps(b, n), o3[:, g, c * 128:(c + 1) * 128], four3[:, g, 0:n], g == 0, g == 3,
                           [("wb", oslot), ("four", g)], [("ps", b)])
                    resid(l, 16, c, bi, s, n, b, i)

            if not last:
                for trig in range(2):
                    for g in range(4):
                        b = trig * 4 + g
                        for lc in range(2):
                            mm(ps(b, 256), ftok3[:, lc, g * 128:(g + 1) * 128], dftc[:, trig, lc, :], lc == 0, lc == 1,
                               [("ftok", lc), ("dftc",)], [("ps", b)])
                        evac(ycs[trig][:, g, 0:256], ps(b, 256), [("ps", b)], [("yc", trig, g)])
                finish(0, 0, 256, 1)
            for bq in range(2):
                s = LC + bq * 512
                for trig in range(2):
                    for p in range(2):
                        dslot = W.request(wload(
                            (lambda trig, p, bq: lambda: dftL_d[trig, p * 1024:(p + 1) * 1024, bq * 512:(bq + 1) * 512]
                             .rearrange("(k q) c -> q k c", q=128))(trig, p, bq), (8, 512), cast=False))
                        d3 = wslot(dslot).rearrange("p (k c) -> p k c", k=8)
                        for g in range(4):
                            b = trig * 4 + g
                            for lc in range(8):
                                t = 2 + 8 * p + lc
                                mm(ps(b), ftok3[:, t, g * 128:(g + 1) * 128], d3[:, lc, :], p == 0 and lc == 0,
                                   p == 1 and lc == 7, [("ftok", t), ("wb", dslot)], [("ps", b)])
                        W.release(dslot)
                    for g in range(4):
                        b = trig * 4 + g
                        evac(ycs[trig][:, g, :], ps(b), [("ps", b)], [("yc", trig, g)])
                finish(1 + bq, s, 512, 0)
                if bq == 0:
                    bn_ = nb()
                    for g in range(4):
                        for t in range(2, 18):
                            mm(PS[:, bn_, g:g + 1], ftok3[:, t, g * 128:(g + 1) * 128], sgn_m[:, t - 2:t - 1], t == 2, t == 17,
                               [("ftok", t), ("cmat",)], [("ps", bn_)])
                    dcopy(ycs[0][:, :, 0:1], PS[:, bn_, 0:4].rearrange("p (g o) -> p g o", o=1), [("ps", bn_)],
                          [("yc", 0, g) for g in range(4)])
                    S.op("dve", lambda e: e.memset(ycs[1][:, :, 0:1], 0.0), [], [("yc", 1, g) for g in range(4)])
                finish(3 + bq, LC + 1024 + bq * 512, 512, 0, True)
            W.release(oslot)

        def attn_pass(l, hp, last, pending, unit_hook=None):
            qT3 = Rb(0, 4608).rearrange("p (c t) -> p c t", c=2)
            kT3 = Rb(4608, 4608).rearrange("p (c t) -> p c t", c=2)
            V3 = Rb(9216, 4608).rearrange("p (t c) -> p t c", t=18)
            attnA3 = Rb(13824, 4608).rearrange("p (c t) -> p c t", c=2)
            COSv = Rf(18432, 2048)
            SINv = Rf(22528, 2048)
            sqv = Rb(26624, 512)
            rstdv = Rf(27136, 512)
            t1v = Rf(28160, 512)
            t2v = Rf(29184, 512)
            PT = [Rb(18432, 1024).rearrange("p (m n) -> p m n", m=2), Rb(19456, 1024).rearrange("p (m n) -> p m n", m=2),
                  Rb(24576, 1024).rearrange("p (m n) -> p m n", m=2)]
            r1v = Rf(20480, 512)
            r2v = Rf(21504, 512)
            rr3 = Rf(20480, 1024).rearrange("p (m n) -> p m n", m=2)
            u1v = Rf(22528, 512)
            u2v = Rf(23552, 512)
            osq = Rb(24576, 512)
            orstd = Rf(25088, 512)

            S.op("sp", lambda e: e.dma_start(out=COSv, in_=cossin_d[0]), list(bar["keys"]), [("cos",)], stream=("cs", 0))
            S.op("sp", lambda e: e.dma_start(out=SINv, in_=cossin_d[1]), list(bar["keys"]), [("sin",)], stream=("cs", 1))

            slot = W.request(wload(lambda: w_in_d[l, :, 1024 + hp * 256:1024 + (hp + 1) * 256].rearrange("(k p) c -> p k c", p=128), (8, 256)))
            s3 = wslot(slot)[:, 0:2048].rearrange("p (k c) -> p k c", k=8)
            for t in range(18):
                b = nb()
                for k in range(KC):
                    mm(ps(b, 256), hT3[:, k, t * 128:(t + 1) * 128], s3[:, k, :], k == 0, k == KC - 1,
                       [("hT", k, blk_of_tile(t)), ("wb", slot)], [("ps", b)])
                evac(V3[:, t, :], ps(b, 256), [("ps", b)], [("V", t)])
            W.release(slot)
            if stop_after == "attn%d_v" % hp:
                raise StopBuild()

            pend = {"q": list(pending), "stage_b": None}
            for which in ("q", "k"):
                base = 0 if which == "q" else 512
                g1c = 64 if which == "q" else 66
                dst3 = qT3 if which == "q" else kT3
                slot = W.request(wload((lambda base: lambda: w_in_d[l, :, base + hp * 256:base + (hp + 1) * 256]
                                        .rearrange("(k p) c -> p k c", p=128))(base), (8, 256)))
                wq = wslot(slot)[:, 0:2048]
                wsw = wslot(slot)[:, 2048:4096]
                w5 = wq.rearrange("p (k a two s) -> p k a two s", k=8, a=8, two=2, s=16)
                sw5 = wsw.rearrange("p (k a two s) -> p k a two s", k=8, a=8, two=2, s=16)
                s3 = wq.rearrange("p (k c) -> p k c", k=8)
                sw3 = wsw.rearrange("p (k c) -> p k c", k=8)
                ulist = [(cl, bi, s, n) for cl in range(2) for bi, (s, n) in enumerate(BLOCKS)
                         if not (which == "q" and bi == 0 and last)]
                ust = {}

                def stage_sq(ui):
                    cl, bi, s, n = ulist[ui]
                    act(sqv[:, 0:n], ps(ust[ui]["bz"], n), AF.Square, [("ps", ust[ui]["bz"])], [("qsq",)], scale=0.125)

                def stage_a(ui):
                    cl, bi, s, n = ulist[ui]
                    if pend["stage_b"] is not None:
                        subln_b(l, pend["stage_b"])
                        pend["stage_b"] = None
                    if pend["q"]:
                        u_ = pend["q"].pop(0)
                        subln_a(l, u_)
                        pend["stage_b"] = u_
                    if ui >= 1:
                        stage_sq(ui - 1)
                    bz = nb()
                    ust[ui] = {"bz": bz}
                    for k in range(KC):
                        mm(ps(bz, n), s3[:, k, cl * 128:(cl + 1) * 128], hT3[:, k, s:s + n], k == 0, k == KC - 1,
                           [("wb", slot), ("hT", k, bi)], [("ps", bz)])
                    if bi > 0:
                        S.op("act", (lambda ui, n, bz: lambda e: e.activation(
                            out=ZB[:, (ui % 2) * 512:(ui % 2) * 512 + n], in_=ps(bz, n), func=AF.Copy))(ui, n, bz),
                            [("ps", bz)], [("zb", ui % 2)])

                def stage_b(ui):
                    cl, bi, s, n = ulist[ui]
                    bz = ust[ui]["bz"]
                    if bi > 0:
                        bs = nb()
                        mm(ps(bs, n), pm_m, ZB[:, (ui % 2) * 512:(ui % 2) * 512 + n], True, True,
                           [("zb", ui % 2), ("cmat",)], [("ps", bs)])
                    bss = nb()
                    mm(ps(bss, n), bones_m, sqv[:, 0:n], True, True, [("qsq",), ("cmat",)], [("ps", bss)])
                    rsqrt_from_psum(rstdv[:, 0:n], ps(bss, n), [("ps", bss), ("eps",)], ("qrs",))
                    okey = (which + "T", cl, bi)
                    if bi == 0:
                        stt(dst3[:, cl, s:s + n], ps(bz, n), vcol(l, g1c), rstdv[:, 0:n], ALU.mult, ALU.mult,
                            [("ps", bz), ("vecs",), ("qrs",)], [okey])
                    else:
                        stt(t1v, ps(bz, n), vcol(l, g1c), COSv[:, s - LC:s - LC + n], ALU.mult, ALU.mult,
                            [("ps", bz), ("vecs",), ("cos",)], [("qt1",)])
                        stt(t2v, ps(bs, n), vcol(l, g1c + 1), SINv[:, s - LC:s - LC + n], ALU.mult, ALU.mult,
                            [("ps", bs), ("vecs",), ("sin",)], [("qt2",)])
                        tt(t1v, t1v, t2v, ALU.add, [("qt1",), ("qt2",)], [("qt1",)])
                        tt(dst3[:, cl, s:s + n], t1v, rstdv[:, 0:n], ALU.mult, [("qt1",), ("qrs",)], [okey])
                    if unit_hook is not None:
                        unit_hook()

                nu = len(ulist)
                for ui in range(nu):
                    stage_a(ui)
                    if ui >= 1:
                        stage_b(ui - 1)
                stage_sq(nu - 1)
                stage_b(nu - 1)
                for tw in range(2):
                    pass
                W.release(slot)

            while pend["stage_b"] is not None or pend["q"]:
                if pend["stage_b"] is not None:
                    subln_b(l, pend["stage_b"])
                    pend["stage_b"] = None
                if pend["q"]:
                    u_ = pend["q"].pop(0)
                    subln_a(l, u_)
                    pend["stage_b"] = u_
            if stop_after == "attn%d_p3" % hp:
                raise StopBuild()
            barrier()
            steps = []
            units = []
            for cl in range(2):
                groups = []
                if not last:
                    groups.append((0, 0, 256, [0, 1]))
                for bi in range(1, 5):
                    groups.append((bi, BLOCKS[bi][0], 512, list(range(18))))
                for (bi, s, n, tiles) in groups:
                    for j, kt in enumerate(tiles):
                        steps.append(dict(cl=cl, bi=bi, s=s, n=n, kt=kt, first=(j == 0), last=(j == len(tiles) - 1)))
            cnt = {"p": 0}

            def kT_keys(cl, kt):
                return [("kT", cl, blk_of_tile(kt))]

            def emitS(st):
                p = cnt["p"] % 2
                st["pt"] = cnt["p"] % 3
                cnt["p"] += 1
                st["p"] = p
                cl, s, n, kt = st["cl"], st["s"], st["n"], st["kt"]
                for m in range(2):
                    b = 2 * p + m
                    mm(ps(b, n), kT3[64 * m:64 * m + 64, cl, kt * 128:(kt + 1) * 128], qT3[64 * m:64 * m + 64, cl, s:s + n],
                       True, True, kT_keys(cl, kt) + [("qT", cl, st["bi"])], [("ps", b)])
                pt = st["pt"]
                act(PT[pt][:, :, 0:n], PS[:, 2 * p:2 * p + 2, 0:n], AF.Exp, [("ps", 2 * p), ("ps", 2 * p + 1)],
                    [("PT", pt)], scale=0.125)

            def emitPV(st):
                p, cl, n, kt = st["pt"], st["cl"], st["n"], st["kt"]
                for m in range(2):
                    mm(ps(4 + m, n), V3[:, kt, cl * 128:(cl + 1) * 128], PT[p][:, m, 0:n], st["first"], st["last"],
                       [("V", kt), ("PT", p)], [("ps", 4 + m)])
                for m in range(2):
                    mm(ps(6 + m, n), ones_m, PT[p][:, m, 0:n], st["first"], st["last"], [("cmat",), ("PT", p)], [("ps", 6 + m)])

            def emitPost(st):
                cl, bi, s, n = st["cl"], st["bi"], st["s"], st["n"]
                dcopy(u1v[:, 0:n], ps(4, n), [("ps", 4)], [("u1",)])
                dcopy(u2v[:, 0:n], ps(5, n), [("ps", 5)], [("u2",)])
                act(rr3[:, :, 0:n], PS[:, 6:8, 0:n], AF.Ln, [("ps", 6), ("ps", 7)], [("r12",)])
                act(rr3[:, :, 0:n], rr3[:, :, 0:n], AF.Exp, [("r12",)], [("r12",)], scale=-1.0)
                tt(u1v[:, 0:n], u1v[:, 0:n], r1v[:, 0:n], ALU.mult, [("u1",), ("r12",)], [("u1",)])
                tt(u2v[:, 0:n], u2v[:, 0:n], r2v[:, 0:n], ALU.mult, [("u2",), ("r12",)], [("u2",)])
                if hp == 0:
                    dst = attnA3[:, cl, s:s + n]
                    okey = ("attnA", cl, bi)
                else:
                    dst = hT3[:, 2 + cl, s:s + n]
                    okey = ("hT", 2 + cl, bi)
                stt(dst, u2v[:, 0:n], neglam[:, l:l + 1], u1v[:, 0:n], ALU.mult, ALU.add,
                    [("u1",), ("u2",), ("neglam", l)], [okey])
                units.append((dst, okey, n))

            ns = len(steps)
            for j in range(min(2, ns)):
                emitS(steps[j])
            for j in range(ns):
                if j + 2 < ns:
                    emitS(steps[j + 2])
                emitPV(steps[j])
                if steps[j]["last"]:
                    emitPost(steps[j])
            return units

        sqb = SUB[:, 0:512]
        rsb = SUB[:, 512:1536].bitcast(F32)

        def subln_a(l, u):
            dst, okey, n = u
            act(sqb[:, 0:n], dst, AF.Square, [okey], [("sqb",)], scale=1.0 / math.sqrt(128.0))

        def subln_b(l, u):
            dst, okey, n = u
            b = nb()
            mm(ps(b, n), ones_m, sqb[:, 0:n], True, True, [("sqb",), ("cmat",)], [("ps", b)])
            rsqrt_from_psum(rsb[:, 0:n], ps(b, n), [("ps", b), ("eps",)], ("rsb",))
            stt(dst, dst, sgc[:, l:l + 1], rsb[:, 0:n], ALU.mult, ALU.mult, [okey, ("rsb",), ("sgc", l)], [okey])

        def outproj_attn(l, last, pending, hook=None):
            attnA3 = Rb(13824, 4608).rearrange("p (c t) -> p c t", c=2)
            slot = W.request(wload(lambda: w_out_d[l, 0:512, :].rearrange("(k p) c -> p k c", p=128), (4, 1024)))
            o3 = wslot(slot).rearrange("p (k c) -> p k c", k=4)
            byblk = {}
            for u in pending:
                byblk.setdefault(u[1][2], []).append(u)
            oblks = [bi for bi in range(5) if not (last and bi == 0)]

            def do_subln(bi):
                for u in byblk.get(bi, []):
                    subln_a(l, u)
                    subln_b(l, u)

            do_subln(oblks[0])
            for oi, bi in enumerate(oblks):
                s, n = BLOCKS[bi]
                if oi + 1 < len(oblks):
                    do_subln(oblks[oi + 1])
                i = 1 if bi == 0 else 0
                for c in range(KC):
                    b = nb()
                    for j in range(4):
                        if j < 2:
                            rhs = attnA3[:, j, s:s + n]
                            rk = ("attnA", j, bi)
                        else:
                            rhs = hT3[:, j, s:s + n]
                            rk = ("hT", j, bi)
                        mm(ps(b, n), o3[:, j, c * 128:(c + 1) * 128], rhs, j == 0, j == 3, [("wb", slot), rk], [("ps", b)])
                    resid(l, 16, c, bi, s, n, b, i)
                if hook is not None:
                    hook(oi)
            W.release(slot)

        def ffn(l, last, between_parts=None, ne=None, ne_next=None):
            uT3 = Rb(0, 8 * T).rearrange("p (j t) -> p j t", j=8)
            sgt = [Rb(8 * T, 512), Rb(8 * T + 512, 512)]
            parts = [(0, 8), (8, 15), (15, 22)]
            fl = {"i": 0}
            for pi, (j0, j1) in enumerate(parts):
                g0 = j0
                while g0 < j1:
                    g1 = min(g0 + 4, j1)
                    ncol = (g1 - g0) * 128
                    gslot = W.request(wload((lambda g0, g1: lambda: w_gate_d[l, :, g0 * 128:g1 * 128]
                                             .rearrange("(k p) c -> p k c", p=128))(g0, g1), (8, ncol)))
                    uslot = W.request(wload((lambda g0, g1: lambda: w_up_d[l, :, g0 * 128:g1 * 128]
                                             .rearrange("(k p) c -> p k c", p=128))(g0, g1), (8, ncol)))
                    gs3 = wslot(gslot)[:, 0:8 * ncol].rearrange("p (k c) -> p k c", k=8)
                    us3 = wslot(uslot)[:, 0:8 * ncol].rearrange("p (k c) -> p k c", k=8)
                    first_grp = (pi == 0 and g0 == 0 and ne is not None)
                    order = []
                    if first_grp:
                        for bi, (s, n) in enumerate(BLOCKS):
                            for j in range(g0, g1):
                                order.append((j, bi, s, n, j == g0))
                    else:
                        for j in range(g0, g1):
                            for bi, (s, n) in enumerate(BLOCKS):
                                order.append((j, bi, s, n, False))
                    for (j, bi, s, n, hook) in order:
                        jc = j - g0
                        if True:
                            if last and bi == 0:
                                continue
                            if hook:
                                ne.block_bi(bi)
                            bg = nb()
                            for k in range(KC):
                                mm(ps(bg, n), gs3[:, k, jc * 128:(jc + 1) * 128], hT3[:, k, s:s + n], k == 0, k == KC - 1,
                                   [("wb", gslot), ("hT", k, bi)], [("ps", bg)])
                            bu = nb()
                            for k in range(KC):
                                mm(ps(bu, n), us3[:, k, jc * 128:(jc + 1) * 128], hT3[:, k, s:s + n], k == 0, k == KC - 1,
                                   [("wb", uslot), ("hT", k, bi)], [("ps", bu)])
                            f = fl["i"]
                            fl["i"] ^= 1
                            act(sgt[f][:, 0:n], ps(bg, n), AF.Silu, [("ps", bg)], [("sgt", f)])
                            tt(uT3[:, j - j0, s:s + n], sgt[f][:, 0:n], ps(bu, n), ALU.mult, [("sgt", f), ("ps", bu)],
                               [("uT", j - j0, bi)])
                    W.release(gslot)
                    W.release(uslot)
                    g0 = g1
                dsl = []
                d0 = j0
                while d0 < j1:
                    d1 = min(d0 + 4, j1)
                    dslot = W.request(wload((lambda d0, d1: lambda: w_down_d[l, d0 * 128:d1 * 128, :]
                                             .rearrange("(k p) c -> p k c", p=128))(d0, d1), (d1 - d0, 1024)))
                    dsl.append((d0, d1, dslot, wslot(dslot)[:, 0:(d1 - d0) * 1024].rearrange("p (k c) -> p k c", k=d1 - d0)))
                    d0 = d1
                for bi, (s, n) in enumerate(BLOCKS):
                    if last and bi == 0:
                        continue
                    i = 1 if bi == 0 else 0
                    for c in range(KC):
                        b = nb()
                        for (d0, d1, dslot, d3) in dsl:
                            for j in range(d0, d1):
                                mm(ps(b, n), d3[:, j - d0, c * 128:(c + 1) * 128], uT3[:, j - j0, s:s + n], j == j0, j == j1 - 1,
                                   [("wb", dslot), ("uT", j - j0, bi)], [("ps", b)])
                        resid(l, 40, c, bi, s, n, b, i)
                    if pi == len(parts) - 1 and ne_next is not None:
                        ne_next.start()
                        ne_next.block_now(bi)
                    if pi == len(parts) - 1 and last and bi > 0:
                        S.op("sp", (lambda s, bi: lambda e: e.dma_start(
                            out=yT_d.rearrange("(k p) t -> p k t", p=128)[:, :, s - LC:s - LC + 512],
                            in_=xT3[:, :, s:s + 512]))(s, bi),
                            [("xT", k, bi) for k in range(KC)], [("y", bi)], stream=("y", bi))
                        out_keys.append(("y", bi))
                for (d0, d1, dslot, d3) in dsl:
                    W.release(dslot)
                if between_parts is not None:
                    between_parts(pi)

        out_keys = []

        def stop(name):
            return stop_after == name

        done = False
        for l in range(DEPTH):
          try:
                last = l == DEPTH - 1
                if l == 0:
                    for j in range(4):
                        ada_piece(0, j)
                    for bi in range(2, 5):
                        load_x_block(bi, [("mod", 0, 2, 0), ("mod", 0, 2, 1)])
                if l == 0:
                    derive_a(l, 1)
                    barrier()
                    for j in range(4, 6):
                        ada_piece(0, j)
                    fourier(l, last, NormEmitter(l, 1, last))
                else:
                    fourier(l, last, ne1_next)
                if stop("fourier"):
                    done = True
                    break
                pending = []
                adaq = {"q": [], "n": 0}
                if l == 0:
                    adaq["q"] = [(0, j) for j in range(6, 12)] + [(1, j) for j in range(12)]

                def ada_pop(k_=1):
                    for _ in range(k_):
                        if adaq["q"]:
                            ll, jj = adaq["q"].pop(0)
                            ada_piece(ll, jj)
                            if (ll, jj) == (1, 3):
                                derive_a(1, 1)

                def unit_hook():
                    adaq["n"] += 1
                    if adaq["n"] % 3 == 0:
                        ada_pop()

                for hp in range(2):
                    barrier()
                    pending = attn_pass(l, hp, last, pending, unit_hook if l == 0 else None)
                    if stop("attn%d" % hp):
                        done = True
                        break
                if done:
                    break
                if l == 0:
                    def ohook(oi):
                        left = len(adaq["q"])
                        ada_pop((left + (4 - oi)) // (5 - oi) if oi < 4 else left)
                    while adaq["q"] and adaq["q"][0][0] == 0:
                        ada_pop()
                    outproj_attn(l, last, pending, ohook)
                else:
                    outproj_attn(l, last, pending)
                if stop("outproj"):
                    done = True
                    break
                derive_a(l, 2)
                ne2 = NormEmitter(l, 2, last)
                ne2.start()
                if l == 0:
                    ne1_next = NormEmitter(1, 1, True)

                    ffn(l, last, None, ne2, ne1_next)
                else:
                    ffn(l, last, None, ne2)
                if stop("layer0"):
                    done = True
                    break
          except StopBuild:
            break

        okeys = list(out_keys)
        if not okeys:
            for k in range(KC):
                S.op("sp", (lambda k: lambda e: e.dma_start(out=yT_d[k * 128:(k + 1) * 128, :], in_=xT3[:, k, LC:T]))(k),
                     [("xT", k, bi) for bi in range(1, 5)], [("y", k)], stream=("y", k))
                okeys.append(("y", k))
        if debug:
            barrier()
            for name, (apfn, keys, shape, dt) in debug.items():
                dd = dbg_d[name]
                S.op("sp", (lambda apfn, dd: lambda e: e.dma_start(out=dd, in_=apfn(locals_)))(apfn, dd), list(bar["keys"]), [("dbg", name)],
                     stream=("dbg", name))
                okeys.append(("dbg", name))
        S.op("sp", lambda e: None, okeys, [])

    EPS_AP = SM[:, 430:431]
    dbg_d = {}
    if debug:
        for name, (apfn, keys, shape, dt) in debug.items():
            dbg_d[name] = nc.dram_tensor("dbg_" + name, list(shape), dt, kind="ExternalOutput").ap()
    locals_ = dict(xT3=xT3, hT3=hT3, R=R, SM=SM, mod4=mod4, a1v=a1v, Rb=Rb, Rf=Rf)

    state["bank"] = 0
    rec = WBM(Sched(), None)
    build(Sched(), rec)
    state["bank"] = 0
    S = Sched()
    W = WBM(S, rec.rec)
    build(S, W)
    with nc.Block() as block:
        S.emit(nc, block)
    return nc


def _perm64():
    d = np.arange(64)
    return np.where((d % 32) < 16, d + 16, d - 16)


def _tok_of_pos():
    tp = np.arange(L)
    m = np.arange(1, L // 2)
    tp[L // 2 + m] = L - m
    return tp


def _host_consts():
    inv = (10000.0 ** (-np.arange(0, 32, 2, dtype=np.float32) / np.float32(32))).astype(np.float32)
    tok = np.arange(L)
    row = (tok // 64).astype(np.float32)
    col = (tok % 64).astype(np.float32)
    cos = np.zeros((128, L), np.float32)
    sin = np.zeros((128, L), np.float32)
    for p in range(128):
        d = p % 64
        pos = row if d < 32 else col
        ang = (pos * inv[d % 16]).astype(np.float32)
        sign = -1.0 if (d % 32) < 16 else 1.0
        cos[p] = np.cos(ang)
        sin[p] = sign * np.sin(ang)
    tp = _tok_of_pos()
    cossin = np.stack([cos[:, tp], sin[:, tp]]).astype(np.float32)

    def dft(n):
        a = np.arange(n, dtype=np.int64)
        m = (np.outer(a, a) % n).astype(np.float64) * (2.0 * np.pi / n)
        return np.cos(m) / np.sqrt(n), np.sin(m) / np.sqrt(n)

    cl, sl = dft(L)
    dftL = np.stack([cl[tp][:, :L // 2], sl[tp][:, :L // 2]]).astype(ml_dtypes.bfloat16)
    sgn = (np.where(tp % 2 == 0, 1.0, -1.0) / np.sqrt(L)).reshape(16, 128).T
    cc_, sc_ = dft(LC)
    dc = np.stack([cc_, sc_])
    dftC = dc.reshape(2, 2, 128, 256).transpose(2, 0, 1, 3).reshape(128, 1024).astype(ml_dtypes.bfloat16)
    c128, s128 = dft(128)
    ones = np.ones((128, 128))
    bones = np.zeros((128, 128))
    bones[:64, :64] = 1
    bones[64:, 64:] = 1
    p64 = _perm64()
    partner = np.concatenate([p64, 64 + p64])
    pm = np.zeros((128, 128))
    pm[partner, np.arange(128)] = 1.0
    cmat = np.concatenate([ones, bones, c128, -s128, s128, sgn, pm], axis=1).astype(ml_dtypes.bfloat16)
    return cossin, dftL, dftC, cmat


def _prep_inputs(inp):
    f32 = np.float32
    cossin, dftL, dftC, cmat = _host_consts()
    perm = _perm64()
    tp = _tok_of_pos()
    shared = {
        "cossin": cossin, "dftL": dftL, "dftC": dftC, "cmat": cmat,
        "w_ada": np.ascontiguousarray(inp["w_ada"], f32), "w_in": np.ascontiguousarray(inp["w_in"], f32),
        "w_out": np.ascontiguousarray(inp["w_out"], f32), "w_gate": np.ascontiguousarray(inp["w_gate"], f32),
        "w_up": np.ascontiguousarray(inp["w_up"], f32), "w_down": np.ascontiguousarray(inp["w_down"], f32),
        "w_fourier": np.ascontiguousarray(inp["w_fourier"], f32),
    }
    lamin = np.zeros((128, 2, 4, 64), f32)
    for l in range(DEPTH):
        for j, nm in enumerate(["lambda_q1", "lambda_k1", "lambda_q2", "lambda_k2"]):
            lamin[:, l, j, :] = np.asarray(inp[nm], f32)[l][None, :]
    shared["lamin"] = lamin.reshape(128, 512)
    maps = []
    for b in range(8):
        vecs = np.zeros((128, NV), f32)
        for l in range(DEPTH):
            o = 69 * l
            vecs[:, o:o + 48] = np.asarray(inp["b_ada"], f32)[l].reshape(48, 128).T
            vecs[:, o + 48:o + 56] = np.asarray(inp["norm1_g"], f32)[l].reshape(8, 128).T
            vecs[:, o + 56:o + 64] = np.asarray(inp["norm2_g"], f32)[l].reshape(8, 128).T
            qg = np.asarray(inp["q_norm_g"], f32)[l]
            kg = np.asarray(inp["k_norm_g"], f32)[l]
            vecs[:, o + 64] = np.tile(qg, 2)
            vecs[:, o + 65] = np.tile(qg[perm], 2)
            vecs[:, o + 66] = np.tile(kg, 2)
            vecs[:, o + 67] = np.tile(kg[perm], 2)
            vecs[:, o + 68] = np.asarray(inp["subln_g"], f32)[l]
        vecs[:, 138:146] = np.asarray(inp["c"], f32)[b].reshape(8, 128).T
        vecs[:, 146:154] = np.asarray(inp["c_ctx"], f32).reshape(8, 128).T
        m = dict(shared)
        m["xT"] = np.ascontiguousarray(np.asarray(inp["x"], f32)[b][tp].T)
        m["cT"] = np.ascontiguousarray(np.asarray(inp["ctx"], f32)[b].T)
        m["vecs"] = vecs
        maps.append(m)
    return maps


_NC_CACHE = {}


def kernel(**inputs):
    maps = _prep_inputs(inputs)
    if "nc" not in _NC_CACHE:
        _NC_CACHE["nc"] = build_program()
    nc = _NC_CACHE["nc"]
    res = run_bass_kernel_spmd(nc, maps, core_ids=list(range(8)))
    tp = _tok_of_pos()
    out = np.empty((8, L, D), np.float32)
    for b, r in enumerate(res.results):
        out[b][tp] = r["yT"].T
    return out
```
